# Optimizing a Trainium2 kernel written in Bass

```python
import jax, jax.numpy as jnp
from jax import lax
import numpy as np

D_MODEL = 1024
BATCH = 2
SEQ = 8192
DEPTH = 4
DEC_BATCH = 1
DEC_SEQ = 16384
PAST_LEN = 128

N_MIXERS = 3
N_LAYERS_A = (DEPTH + 2) // 3
N_LAYERS_B = (DEPTH + 1) // 3
N_LAYERS_C = DEPTH // 3
CHUNK = 128
A_WIDTH = D_MODEL
A_GROUPS = 8
A_HEAD = A_WIDTH // A_GROUPS
B_HEADS = 16
B_KV_HEADS = 4
B_HEAD_DIM = 64
B_Q_PER_KV = B_HEADS // B_KV_HEADS
WINDOW = 128
BLOCK = 128
C_WIDTH = D_MODEL
C_GROUPS = 8
C_GROUP_DIM = C_WIDTH // C_GROUPS
FF_DIM = 2816
CONV_WIDTH = 3
EPS = 1e-6
NEG_INF = -1e30

kernel_name = "hybrid_bidir_gmlp_swa_fnet_encoder"


def rmsnorm(x, g):
    xf = x.astype(jnp.float32)
    y = xf * lax.rsqrt(jnp.mean(xf * xf, axis=-1, keepdims=True) + EPS)
    return y.astype(x.dtype) * g


def alibi_slopes():
    return jnp.exp2(-8.0 * jnp.arange(1, B_HEADS + 1, dtype=jnp.float32) / B_HEADS)


def gmlp_chunk_mixer(h, w_in, g_v, w_s, b_s, w_out):
    b, s, _ = h.shape
    uv = jax.nn.gelu(h @ w_in, approximate=False)
    u, v = jnp.split(uv, 2, axis=-1)
    v = rmsnorm(v, g_v).reshape(b, s // CHUNK, CHUNK, A_GROUPS, A_HEAD)
    sv = jnp.einsum("gts,bnsgd->bntgd", w_s, v) + b_s.T[None, None, :, :, None]
    return (u * sv.reshape(b, s, A_WIDTH)) @ w_out


def windowed_gqa_mixer(h, w_qkv, sinks, w_o):
    b, s, _ = h.shape
    nb = s // BLOCK
    qkv = h @ w_qkv
    q, k, v = jnp.split(qkv, [B_HEADS * B_HEAD_DIM, (B_HEADS + B_KV_HEADS) * B_HEAD_DIM], axis=-1)
    q = q.reshape(b, nb, BLOCK, B_KV_HEADS, B_Q_PER_KV, B_HEAD_DIM)

    def band(t):
        t = t.reshape(b, s, B_KV_HEADS, B_HEAD_DIM)
        tp = jnp.pad(t, ((0, 0), (BLOCK, BLOCK), (0, 0), (0, 0)))
        tp = tp.reshape(b, nb + 2, BLOCK, B_KV_HEADS, B_HEAD_DIM)
        return jnp.concatenate([tp[:, :-2], tp[:, 1:-1], tp[:, 2:]], axis=2)

    kb, vb = band(k), band(v)
    scores = jnp.einsum("bnqkgd,bnskd->bnkgqs", q, kb,
                        preferred_element_type=jnp.float32) * (B_HEAD_DIM ** -0.5)
    qpos = jnp.arange(nb)[:, None, None] * BLOCK + jnp.arange(BLOCK)[None, :, None]
    kpos = (jnp.arange(nb)[:, None, None] - 1) * BLOCK + jnp.arange(3 * BLOCK)[None, None, :]
    dist = jnp.abs(qpos - kpos)
    valid = (dist <= WINDOW) & (kpos >= 0) & (kpos < s)
    slopes = alibi_slopes().reshape(B_KV_HEADS, B_Q_PER_KV)
    alibi = -slopes[None, :, :, None, None] * dist.astype(jnp.float32)[:, None, None]
    scores = jnp.where(valid[None, :, None, None], scores + alibi[None], NEG_INF)
    sink = jnp.broadcast_to(
        sinks.astype(jnp.float32).reshape(B_KV_HEADS, B_Q_PER_KV)[None, None, :, :, None, None],
        scores.shape[:-1] + (1,))
    p = jax.nn.softmax(jnp.concatenate([scores, sink], axis=-1), axis=-1)[..., :-1]
    o = jnp.einsum("bnkgqs,bnskd->bnqkgd", p.astype(vb.dtype), vb)
    return o.reshape(b, s, B_HEADS * B_HEAD_DIM) @ w_o


def fourier_mixer(h, w_in, w_out):
    b, s, _ = h.shape
    z = (h @ w_in).reshape(b, s, C_GROUPS, C_GROUP_DIM).astype(jnp.float32)
    f = jnp.real(jnp.fft.fft2(z, axes=(1, 3), norm="ortho")).astype(h.dtype)
    return f.reshape(b, s, C_WIDTH) @ w_out


def conv_gated_ffn(h, w_up, w_conv, b_conv, w_down):
    s = h.shape[1]
    up = h @ w_up
    p = jnp.pad(up, ((0, 0), (1, 1), (0, 0)))
    up = p[:, :s] * w_conv[0] + p[:, 1:s + 1] * w_conv[1] + p[:, 2:] * w_conv[2] + b_conv
    a, g = jnp.split(up, 2, axis=-1)
    return (a * jax.nn.silu(g)) @ w_down


def trunk(x, c, w_ada, b_ada, norm_g, a_w_in, a_g_v, a_w_s, a_b_s, a_w_out,
          b_w_qkv, b_sinks, b_w_o, c_w_in, c_w_out,
          f_w_up, f_w_conv, f_b_conv, f_w_down, g_final):
    cs = jax.nn.silu(c)
    for i in range(DEPTH):
        mod = cs @ w_ada[i] + b_ada[i]
        sh1, sc1, gt1, sh2, sc2, gt2 = jnp.split(mod[:, None, :], 6, axis=-1)
        h = rmsnorm(x, norm_g[i, 0]) * (1 + sc1) + sh1
        kind, j = i % N_MIXERS, i // N_MIXERS
        if kind == 0:
            m = gmlp_chunk_mixer(h, a_w_in[j], a_g_v[j], a_w_s[j], a_b_s[j], a_w_out[j])
        elif kind == 1:
            m = windowed_gqa_mixer(h, b_w_qkv[j], b_sinks[j], b_w_o[j])
        else:
            m = fourier_mixer(h, c_w_in[j], c_w_out[j])
        x = x + gt1 * m
        h = rmsnorm(x, norm_g[i, 1]) * (1 + sc2) + sh2
        x = x + gt2 * conv_gated_ffn(h, f_w_up[i], f_w_conv[i], f_b_conv[i], f_w_down[i])
    return rmsnorm(x, g_final)


def setup_inputs(seed: int = 0) -> dict:
    key = jax.random.key(seed)
    ks = jax.random.split(key, 24)
    D = D_MODEL
    f32 = jnp.float32

    def nrm(k, shape, scale):
        return jax.random.normal(k, shape, f32) * scale

    qkv_w = (B_HEADS + 2 * B_KV_HEADS) * B_HEAD_DIM
    conv_center = jnp.zeros((CONV_WIDTH, 1), f32).at[1].set(1.0)
    return {
        "x_prompt": nrm(ks[0], (BATCH, SEQ, D), 1.0),
        "x_sample": nrm(ks[1], (DEC_BATCH, DEC_SEQ, D), 1.0),
        "c_prompt": nrm(ks[2], (BATCH, D), 1.0),
        "c_sample": nrm(ks[3], (DEC_BATCH, D), 1.0),
        "w_ada": nrm(ks[4], (DEPTH, D, 6 * D), 0.5 * D ** -0.5),
        "b_ada": nrm(ks[5], (DEPTH, 6 * D), 0.01),
        "norm_g": 1.0 + nrm(ks[6], (DEPTH, 2, D), 0.05),
        "a_w_in": nrm(ks[7], (N_LAYERS_A, D, 2 * A_WIDTH), D ** -0.5),
        "a_g_v": 1.0 + nrm(ks[8], (N_LAYERS_A, A_WIDTH), 0.05),
        "a_w_s": nrm(ks[9], (N_LAYERS_A, A_GROUPS, CHUNK, CHUNK), CHUNK ** -0.5),
        "a_b_s": 1.0 + nrm(ks[10], (N_LAYERS_A, A_GROUPS, CHUNK), 0.1),
        "a_w_out": nrm(ks[11], (N_LAYERS_A, A_WIDTH, D), A_WIDTH ** -0.5),
        "b_w_qkv": nrm(ks[12], (N_LAYERS_B, D, qkv_w), D ** -0.5),
        "b_sinks": nrm(ks[13], (N_LAYERS_B, B_HEADS), 0.5),
        "b_w_o": nrm(ks[14], (N_LAYERS_B, B_HEADS * B_HEAD_DIM, D), (B_HEADS * B_HEAD_DIM) ** -0.5),
        "c_w_in": nrm(ks[15], (N_LAYERS_C, D, C_WIDTH), D ** -0.5),
        "c_w_out": nrm(ks[16], (N_LAYERS_C, C_WIDTH, D), C_WIDTH ** -0.5),
        "f_w_up": nrm(ks[17], (DEPTH, D, 2 * FF_DIM), D ** -0.5),
        "f_w_conv": conv_center[None] + nrm(ks[18], (DEPTH, CONV_WIDTH, 2 * FF_DIM), 0.2),
        "f_b_conv": nrm(ks[19], (DEPTH, 2 * FF_DIM), 0.01),
        "f_w_down": nrm(ks[20], (DEPTH, FF_DIM, D), FF_DIM ** -0.5),
        "g_final": 1.0 + nrm(ks[21], (D,), 0.05),
    }


def reference(x_prompt, x_sample, c_prompt, c_sample, w_ada, b_ada, norm_g,
              a_w_in, a_g_v, a_w_s, a_b_s, a_w_out, b_w_qkv, b_sinks, b_w_o,
              c_w_in, c_w_out, f_w_up, f_w_conv, f_b_conv, f_w_down, g_final):
    y_prompt = trunk(x_prompt, c_prompt, w_ada, b_ada, norm_g, a_w_in, a_g_v, a_w_s, a_b_s,
                     a_w_out, b_w_qkv, b_sinks, b_w_o, c_w_in, c_w_out,
                     f_w_up, f_w_conv, f_b_conv, f_w_down, g_final)
    y_sample = trunk(x_sample, c_sample, w_ada, b_ada, norm_g, a_w_in, a_g_v, a_w_s, a_b_s,
                     a_w_out, b_w_qkv, b_sinks, b_w_o, c_w_in, c_w_out,
                     f_w_up, f_w_conv, f_b_conv, f_w_down, g_final)
    return (y_prompt, y_sample)
```

```python
import numpy as np
import ml_dtypes
import concourse.bass as bass
import concourse.mybir as mybir
from concourse.bass_utils import run_bass_kernel_spmd

F32 = mybir.dt.float32
BF16 = mybir.dt.bfloat16
I32 = mybir.dt.int32
AF = mybir.ActivationFunctionType
ALU = mybir.AluOpType
AX = mybir.AxisListType

D = 1024
FF = 2816
NCORE = 8
EPS = 1e-6
NEG = -1e30


class Tok:
    __slots__ = ("sem", "val", "eng", "buf")

    def __init__(self, sem, val, eng, buf=None):
        self.sem, self.val, self.eng, self.buf = sem, val, eng, buf


class Buf:
    def __init__(self, name):
        self.name = name
        self.lw = None
        self.rd = []
        self.dsem = None
        self.dcnt = 0


class Rec:
    def __init__(self, nc):
        self.nc = nc
        self.E = {"pe": nc.tensor, "dve": nc.vector, "act": nc.scalar, "pool": nc.gpsimd, "sp": nc.sync}
        self.sem = {e: nc.alloc_semaphore("s_" + e) for e in self.E}
        self.cnt = {e: 0 for e in self.E}
        self.seen = {e: {} for e in self.E}
        self.pending = {e: [] for e in self.E}
        self.dbufs = []
        self.nsem = 0
        self.pool = []

    def _getsem(self, sb, prefix):
        if self.pool:
            sb.dsem, sb.dcnt = self.pool.pop()
        else:
            sb.dsem = self.nc.alloc_semaphore("%s_%d" % (prefix, self.nsem))
            self.nsem += 1
            sb.dcnt = 0
        self.dbufs.append(sb)

    def _wait(self, eng, t):
        if t is None:
            return
        if t.buf is not None:
            if t.buf.dsem is not t.sem:
                return
            val = t.buf.dcnt
        else:
            if t.eng == eng and eng == "pe":
                return
            val = t.val
            assert val is not None, "dependency on un-signalled instruction"
        sid = id(t.sem)
        if self.seen[eng].get(sid, 0) >= val:
            return
        self.E[eng].wait_ge(t.sem, val)
        self.seen[eng][sid] = val

    def _deps(self, eng, rd, wr):
        for b in rd:
            self._wait(eng, b.lw)
        for b in wr:
            self._wait(eng, b.lw)
            for t in b.rd:
                self._wait(eng, t)

    def _upd(self, tok, rd, wr):
        for b in wr:
            b.lw = tok
            b.rd = []
        for b in rd:
            b.rd = [t for t in b.rd if t.sem is not tok.sem] + [tok]

    def op(self, eng, fn, rd=(), wr=(), sig=True):
        self._deps(eng, rd, wr)
        ins = fn(self.E[eng])
        tok = Tok(self.sem[eng], None, eng)
        self.pending[eng].append(tok)
        if sig:
            self.cnt[eng] += 1
            ins.then_inc(self.sem[eng], 1)
            for t in self.pending[eng]:
                t.val = self.cnt[eng]
            self.pending[eng] = []
        self._upd(tok, rd, wr)
        return tok

    def dma(self, eng, out, in_, rd=(), wr=(), sb=None):
        self._deps(eng, rd, wr)
        if sb.dsem is None:
            self._getsem(sb, "d")
        ins = self.E[eng].dma_start(out=out, in_=in_)
        sb.dcnt += 16
        ins.then_inc(sb.dsem, 16)
        tok = Tok(sb.dsem, sb.dcnt, None, sb)
        self._upd(tok, rd, wr)
        return tok

    def allgather(self, in_ap, out_ap, rd=(), wr=(), sb=None):
        eng = "pool"
        self._deps(eng, rd, wr)
        if sb.dsem is None:
            self._getsem(sb, "c")
        ins = self.nc.gpsimd.collective_compute(
            "AllGather", ALU.bypass, replica_groups=[list(range(NCORE))], ins=[in_ap.opt()], outs=[out_ap.opt()]
        )
        sb.dcnt += 1
        ins.then_inc(sb.dsem)
        tok = Tok(sb.dsem, sb.dcnt, None, sb)
        self._upd(tok, rd, wr)
        return tok

    def barrier(self):
        for e in self.E:
            assert not self.pending[e], "barrier with un-signalled instructions on " + e
        for e in self.E:
            for o in self.E:
                if o == e:
                    continue
                v = self.cnt[o]
                if v and self.seen[e].get(id(self.sem[o]), 0) < v:
                    self.E[e].wait_ge(self.sem[o], v)
                    self.seen[e][id(self.sem[o])] = v
            for b in self.dbufs:
                if b.dcnt and self.seen[e].get(id(b.dsem), 0) < b.dcnt:
                    self.E[e].wait_ge(b.dsem, b.dcnt)
                    self.seen[e][id(b.dsem)] = b.dcnt
        for b in self.dbufs:
            self.pool.append((b.dsem, b.dcnt))
            b.dsem = None
            b.dcnt = 0
        self.dbufs = []


class Cfg:
    def __init__(self, tpc):
        self.TPC = tpc
        self.NT = tpc // 128
        self.NTOK = tpc * NCORE
        self.seqs = [(0, 2 * tpc), (2 * tpc, 2 * tpc), (4 * tpc, 4 * tpc)]


def bf(a):
    return np.ascontiguousarray(np.asarray(a, np.float32)).astype(ml_dtypes.bfloat16)


def dft_tables(S):
    N1 = S // 128
    s2 = np.arange(128, dtype=np.float64)[:, None, None]
    k1 = np.arange(N1, dtype=np.float64)[None, :, None]
    k2 = np.arange(128, dtype=np.float64)[None, None, :]
    ph = 2 * np.pi * ((s2 * (k1 + N1 * k2)) % S) / S
    c2, s2t = np.cos(ph), -np.sin(ph)
    a = np.arange(N1, dtype=np.float64)
    ph1 = 2 * np.pi * ((a[:, None] * a[None, :]) % N1) / N1
    t1a = np.concatenate([np.cos(ph1), np.sin(ph1)], axis=1)
    t1b = np.concatenate([-np.sin(ph1), np.cos(ph1)], axis=1)
    return bf(c2.reshape(128, N1 * 128)), bf(s2t.reshape(128, N1 * 128)), bf(t1a), bf(t1b)


def build(cfg, debug=None):
    TPC, NT, NTOK = cfg.TPC, cfg.NT, cfg.NTOK
    nc = bass.Bass("TRN2", target_bir_lowering=False)
    R = Rec(nc)

    def din(name, shape, dt=F32):
        return nc.dram_tensor(name, list(shape), dt, kind="ExternalInput").ap()

    def dscr(name, shape, dt):
        return nc.dram_tensor(name, list(shape), dt).ap()

    x_in = din("x", [TPC, D])
    cT = din("cT", [128, 8])
    w_ada = din("w_ada", [4, D, 6 * D])
    b_ada = din("b_ada", [4, 6 * D])
    norm_g = din("norm_g", [4, 2, D])
    a_w_in = din("a_w_in", [2, D, 2 * D])
    a_g_v = din("a_g_v", [2, D])
    a_w_sT = din("a_w_sT", [2, 8, 128, 128])
    a_b_sT = din("a_b_sT", [2, 128, 8])
    a_w_out = din("a_w_out", [2, D, D])
    b_w_qkv = din("b_w_qkv", [D, 1536])
    b_sinks = din("b_sinks", [1, 16])
    b_w_o = din("b_w_o", [D, D])
    c_w_inT = din("c_w_inT", [128, D])
    c_w_out = din("c_w_out", [D, D])
    f_w_up = din("f_w_up", [4, D, 2 * FF])
    f_w_convT = din("f_w_convT", [4, 128, 3, 44])
    f_b_convT = din("f_b_convT", [4, 128, 44])
    f_w_down = din("f_w_down", [4, FF, D])
    g_final = din("g_final", [1, D])
    ident_in = din("ident", [128, 128], BF16)
    m2_in = din("m2", [2, 1])
    mk_in = din("mk", [128, 2])
    mm_in = din("mm", [128, 2])
    adist = din("adist", [128, 384])
    amask = din("amask", [128, 384])
    cs128 = din("cs128", [128, 256], BF16)
    tabs = {}
    for (st, S) in cfg.seqs:
        if S not in tabs:
            N1 = S // 128
            tabs[S] = (din("t2c_%d" % S, [128, N1 * 128], BF16), din("t2s_%d" % S, [128, N1 * 128], BF16),
                       din("t1a_%d" % S, [N1, 2 * N1], BF16), din("t1b_%d" % S, [N1, 2 * N1], BF16))
    y_out = nc.dram_tensor("y", [TPC, D], F32, kind="ExternalOutput").ap()
    dbg_out = None
    if debug:
        dbg_out = nc.dram_tensor("dbg", [TPC, D], F32, kind="ExternalOutput").ap()

    xs = dscr("xs", [TPC, D], F32)
    modrow = dscr("modrow", [4, 6 * D], F32)
    hbuf = dscr("hbuf", [TPC + 2, D], BF16)
    agF_in = dscr("agF_in", [2, D], BF16)
    agF_out = dscr("agF_out", [2 * NCORE, D], BF16)
    cpF = dscr("cpF", [2 * NCORE + 4, D], BF16)
    kvbuf = dscr("kvbuf", [TPC + 256, 512], BF16)
    agK_in = dscr("agK_in", [256, 512], BF16)
    agK_out = dscr("agK_out", [256 * NCORE, 512], BF16)
    cpK = dscr("cpK", [256 * (NCORE + 2), 512], BF16)
    agH_in = dscr("agH_in", [TPC, D], BF16)
    agH_out = dscr("agH_out", [NTOK, D], BF16)
    zbuf = dscr("zbuf", [NTOK, 256], BF16)
    agY_in = dscr("agY_in", [NTOK, 128], BF16)
    agY_out = dscr("agY_out", [NCORE * NTOK, 128], BF16)
    cpY = dscr("cpY", [NCORE * NTOK, 128], BF16)

    fown = dscr("fown", [NCORE, TPC, 128], BF16)
    pid = nc.partition_id([mybir.EngineType.SP])
    pid_act = nc.partition_id([mybir.EngineType.Activation])

    ARENA = 198 * 1024
    arena = nc.alloc_sbuf_tensor("arena", [128, ARENA // 2], BF16)

    class Carver:
        def __init__(self, lo, hi):
            self.lo, self.hi, self.p = lo, hi, lo

        def get(self, shape, dt, name):
            esz = 4 if dt == F32 or dt == I32 else 2
            n = int(np.prod(shape[1:]))
            nb = (n * esz + 31) // 32 * 32
            assert self.p + nb <= self.hi, ("arena overflow", name, self.p, nb, self.hi)
            v = arena[0:shape[0], self.p // 2:(self.p + n * esz) // 2]
            if esz == 4:
                v = v.bitcast(dt)
            self.p += nb
            if len(shape) == 3:
                v = v.rearrange("p (a b) -> p a b", a=shape[1])
            elif len(shape) == 4:
                v = v.rearrange("p (a b c) -> p a b c", a=shape[1], b=shape[2])
            return v, Buf(name)

    WREG = 132 * 1024
    pers = Carver(WREG + 44 * 1024, ARENA)
    ident, b_ident = pers.get([128, 128], BF16, "ident")
    csb, b_csb = pers.get([128, 8, 128], BF16, "csb")
    ones32, b_ones = pers.get([128, 128], F32, "ones")
    modp, b_modp = pers.get([128, 3, D], F32, "modp")
    m2t, b_m2 = pers.get([2, 1], F32, "m2")
    mkt, b_mk = pers.get([128, 2], F32, "mk")
    mmt, b_mm = pers.get([128, 2], F32, "mm")
    small, b_small = pers.get([128, 64], F32, "small")

    pT = [nc.alloc_psum_tensor("pT%d" % i, [128, 1024], BF16).ap() for i in range(2)]
    bpT = [Buf("pT%d" % i) for i in range(2)]
    pM = [nc.alloc_psum_tensor("pM%d" % i, [128, 512], F32).ap() for i in range(6)]
    bpM = [Buf("pM%d" % i) for i in range(6)]
    rr = {"pT": 0, "pM": 0}

    def next_pT():
        i = rr["pT"] % 2
        rr["pT"] += 1
        return pT[i], bpT[i]

    def next_pM():
        i = rr["pM"] % 6
        rr["pM"] += 1
        return pM[i], bpM[i]

    R.dma("sp", ident, ident_in, wr=[b_ident], sb=b_ident)
    R.dma("sp", m2t, m2_in, wr=[b_m2], sb=b_m2)
    R.dma("sp", mkt, mk_in, wr=[b_mk], sb=b_mk)
    R.dma("sp", mmt, mm_in, wr=[b_mm], sb=b_mm)
    R.op("dve", lambda e: e.memset(ones32, 1.0), wr=[b_ones])
    wk = Carver(WREG, WREG + 44 * 1024)
    ct, b_ct = wk.get([128, 8], F32, "ct")
    cs, b_cs = wk.get([128, 8], F32, "cs")
    R.dma("sp", ct, cT, wr=[b_ct], sb=b_ct)
    R.op("act", lambda e: e.activation(out=cs, in_=ct, func=AF.Silu), rd=[b_ct], wr=[b_cs])
    for a in range(8):
        R.op("dve", lambda e, a=a: e.tensor_scalar(out=csb[:, a, :], in0=ones32, scalar1=cs[:, a:a + 1], scalar2=None,
                                                   op0=ALU.mult), rd=[b_ones, b_cs], wr=[b_csb])
    wv = Carver(0, WREG)
    wst = [wv.get([128, 2048], BF16, "wst%d" % i) for i in range(2)]
    mrow, b_mrow = wv.get([128, 2048], F32, "mrow")
    brow, b_brow = wv.get([128, 2048], F32, "brow")
    k = 0
    for l in range(4):
        for grp in range(3):
            cols = slice(grp * 2048, (grp + 1) * 2048)
            R.dma("sp", brow[0:1, :], b_ada[l:l + 1, cols], wr=[b_brow], sb=b_brow)
            ps = [next_pM() for _ in range(4)]
            for a in range(8):
                w, bw = wst[k % 2]
                k += 1
                R.dma("pool", w, w_ada[l, a * 128:(a + 1) * 128, cols], wr=[bw], sb=bw)
                for j in range(4):
                    R.op("pe", lambda e, j=j, a=a, w=w: e.matmul(ps[j][0], lhsT=csb[:, a, :], rhs=w[:, j * 512:(j + 1) * 512],
                                                                 start=(a == 0), stop=(a == 7)),
                         rd=[b_csb, bw], wr=[ps[j][1]], sig=(a == 7 or j == 3))
            for j in range(4):
                R.op("dve", lambda e, j=j: e.tensor_tensor(out=mrow[0:1, j * 512:(j + 1) * 512], in0=ps[j][0][0:1, :],
                                                           in1=brow[0:1, j * 512:(j + 1) * 512], op=ALU.add),
                     rd=[ps[j][1], b_brow], wr=[b_mrow])
            R.dma("sp", modrow[l:l + 1, cols], mrow[0:1, :], rd=[b_mrow], wr=[], sb=b_mrow)
    xt0, b_xt0 = wk.get([128, D], F32, "xt0")
    for q in range(NT):
        R.dma("sp", xt0, x_in[q * 128:(q + 1) * 128, :], wr=[b_xt0], sb=b_xt0)
        R.dma("sp", xs[q * 128:(q + 1) * 128, :], xt0, rd=[b_xt0], sb=b_xt0)
    R.barrier()

    def load_mod(l, which):
        base = which * 3 * D
        wkk = Carver(WREG, WREG + 44 * 1024)
        gt, b_gt = wkk.get([128, D], F32, "gtmp")
        R.dma("sp", modp.rearrange("p a b -> p (a b)"), modrow[l:l + 1, base:base + 3 * D].partition_broadcast(128),
              wr=[b_modp], sb=b_modp)
        R.dma("sp", gt, norm_g[l, which:which + 1, :].partition_broadcast(128), wr=[b_gt], sb=b_gt)
        R.op("dve", lambda e: e.scalar_tensor_tensor(out=modp[:, 1, :], in0=modp[:, 1, :], scalar=1.0, in1=gt,
                                                     op0=ALU.add, op1=ALU.mult), rd=[b_modp, b_gt], wr=[b_modp])
        R.barrier()

    class Work:
        pass

    def norm_tile(W, src_ap, slot, gA=None, gB=None):
        xt, bxt = W.xt[slot % len(W.xt)]
        R.dma("sp", xt, src_ap, wr=[bxt], sb=bxt)
        junk, bj = W.junk
        ss, bss = W.ss[slot % len(W.ss)]
        R.op("act", lambda e: e.activation(out=junk, in_=xt, func=AF.Square, accum_out=ss[:, 0:1]), rd=[bxt], wr=[bj, bss])
        R.op("dve", lambda e: e.tensor_scalar(out=ss[:, 1:2], in0=ss[:, 0:1], scalar1=1.0 / D, scalar2=EPS, op0=ALU.mult,
                                              op1=ALU.add), rd=[bss], wr=[bss])
        R.op("act", lambda e: e.activation(out=ss[:, 2:3], in_=ss[:, 1:2], func=AF.Sqrt), rd=[bss], wr=[bss])
        R.op("dve", lambda e: e.reciprocal(out=ss[:, 3:4], in_=ss[:, 2:3]), rd=[bss], wr=[bss])
        A = modp[:, 1, :] if gA is None else gA
        R.op("dve", lambda e: e.scalar_tensor_tensor(out=junk, in0=xt, scalar=ss[:, 3:4], in1=A, op0=ALU.mult, op1=ALU.mult),
             rd=[bxt, bss, b_modp], wr=[bj])
        hb, bhb = W.hb[slot % len(W.hb)]
        if gB is None:
            R.op("pool", lambda e: e.tensor_tensor(out=hb, in0=junk, in1=modp[:, 0, :], op=ALU.add), rd=[bj, b_modp], wr=[bhb])
        else:
            R.op("pool", lambda e: e.tensor_copy(out=hb, in_=junk), rd=[bj], wr=[bhb])
        return xt, bxt, hb, bhb

    def transpose_tile(src, bsrc, dst, bdst, nrows=128, ncols=128, dcol=None):
        p, bp = next_pT()
        for a in range(8):
            R.op("pe", lambda e, a=a: e.transpose(p[:, a * 128:a * 128 + nrows], src[0:nrows, a * 128:(a + 1) * 128],
                                                  ident[0:nrows, 0:nrows]),
                 rd=[bsrc, b_ident], wr=[bp], sig=(a == 7))
        pv = p.rearrange("p (a t) -> p a t", a=8)[:, :, 0:nrows]
        R.op("act", lambda e: e.activation(out=dst if dcol is None else dcol, in_=pv, func=AF.Copy), rd=[bp], wr=[bdst])

    def load_w(dst, bdst, src, eng="pool"):
        R.dma(eng, dst, src, wr=[bdst], sb=bdst)

    def proj_tm(lhsT, blhs, w, bw, ncol, K=8):
        outs = []
        for c0 in range(0, ncol, 512):
            cw = min(512, ncol - c0)
            p, bp = next_pM()
            for a in range(K):
                R.op("pe", lambda e, a=a, c0=c0, cw=cw, p=p: e.matmul(p[:, 0:cw], lhsT=lhsT[:, a, :], rhs=w[:, a, c0:c0 + cw],
                                                                    start=(a == 0), stop=(a == K - 1)),
                     rd=[blhs, bw], wr=[bp], sig=(a == K - 1))
            outs.append((p, bp, cw, c0))
        return outs

    def residual_out(W, outs, xt, bxt, q, final_dst=None):
        tmp, btmp = W.junk
        for (p, bp, cw, c0) in outs:
            R.op("dve", lambda e, p=p, c0=c0, cw=cw: e.tensor_tensor(out=tmp[:, c0:c0 + cw], in0=p[:, 0:cw],
                                                                      in1=modp[:, 2, c0:c0 + cw], op=ALU.mult),
                 rd=[bp, b_modp], wr=[btmp])
        R.op("pool", lambda e: e.tensor_tensor(out=xt, in0=xt, in1=tmp, op=ALU.add), rd=[btmp, bxt], wr=[bxt])
        R.dma("sp", xs[q * 128:(q + 1) * 128, :], xt, rd=[bxt], sb=bxt)

    def std_work(cv):
        W = Work()
        W.xt = [cv.get([128, D], F32, "xt%d" % i) for i in range(2)]
        W.junk = cv.get([128, D], F32, "junk")
        W.ss = [cv.get([128, 4], F32, "ss%d" % i) for i in range(2)]
        W.hb = [cv.get([128, D], BF16, "hb%d" % i) for i in range(2)]
        W.hT = [cv.get([128, 8, 128], BF16, "hT%d" % i) for i in range(2)]
        return W

    def phase_gmlp(l, j):
        load_mod(l, 0)
        wv = Carver(0, WREG)
        w_in, b_win = wv.get([128, 8, 2 * D], BF16, "a_w_in")
        w_out, b_wout = wv.get([128, 8, D], BF16, "a_w_out")
        w_s, b_ws = wv.get([128, 8, 128], BF16, "a_w_s")
        gv, b_gv = wv.get([128, D], F32, "a_g_v")
        bsT, b_bsT = wv.get([128, 8], F32, "a_b_sT")
        u, b_u = wv.get([128, D], F32, "u")
        v, b_v = wv.get([128, D], F32, "v")
        vn, b_vn = wv.get([128, D], BF16, "vn")
        gb, b_gb = wv.get([128, D], BF16, "gated")
        gT, b_gT = wv.get([128, 8, 128], BF16, "gT")
        vs, b_vs = wv.get([128, 8], F32, "vs")
        for a in range(8):
            load_w(w_in[:, a, :], b_win, a_w_in[j, a * 128:(a + 1) * 128, :])
            load_w(w_out[:, a, :], b_wout, a_w_out[j, a * 128:(a + 1) * 128, :])
            load_w(w_s[:, a, :], b_ws, a_w_sT[j, a, :, :])
        R.dma("sp", gv, a_g_v[j:j + 1, :].partition_broadcast(128), wr=[b_gv], sb=b_gv)
        R.dma("sp", bsT, a_b_sT[j, :, :], wr=[b_bsT], sb=b_bsT)
        W = std_work(Carver(WREG, WREG + 44 * 1024))
        for q in range(NT):
            xt, bxt, hb, bhb = norm_tile(W, xs[q * 128:(q + 1) * 128, :], q)
            hT, bhT = W.hT[q % 2]
            transpose_tile(hb, bhb, hT, bhT)
            outs = proj_tm(hT, bhT, w_in, b_win, 2 * D)
            for i, (p, bp, cw, c0) in enumerate(outs):
                dst, bd = (u, b_u) if i < 2 else (v, b_v)
                cc = c0 % D
                R.op("act", lambda e, p=p, dst=dst, cc=cc: e.activation(out=dst[:, cc:cc + 512], in_=p, func=AF.Gelu),
                     rd=[bp], wr=[bd])
            R.op("act", lambda e: e.activation(out=vn, in_=v, func=AF.Square, accum_out=vs[:, 0:1]), rd=[b_v], wr=[b_vn, b_vs])
            R.op("dve", lambda e: e.tensor_scalar(out=vs[:, 1:2], in0=vs[:, 0:1], scalar1=1.0 / D, scalar2=EPS, op0=ALU.mult,
                                                  op1=ALU.add), rd=[b_vs], wr=[b_vs])
            R.op("act", lambda e: e.activation(out=vs[:, 2:3], in_=vs[:, 1:2], func=AF.Sqrt), rd=[b_vs], wr=[b_vs])
            R.op("dve", lambda e: e.reciprocal(out=vs[:, 3:4], in_=vs[:, 2:3]), rd=[b_vs], wr=[b_vs])
            R.op("dve", lambda e: e.scalar_tensor_tensor(out=vn, in0=v, scalar=vs[:, 3:4], in1=gv, op0=ALU.mult, op1=ALU.mult),
                 rd=[b_v, b_vs, b_gv], wr=[b_vn])
            for half in range(2):
                p, bp = next_pM()
                for gg in range(4):
                    g = half * 4 + gg
                    R.op("pe", lambda e, g=g, gg=gg, p=p: e.matmul(p[:, gg * 128:(gg + 1) * 128], lhsT=w_s[:, g, :],
                                                                  rhs=vn[:, g * 128:(g + 1) * 128], start=True, stop=True),
                         rd=[b_ws, b_vn], wr=[bp], sig=(gg == 3))
                for gg in range(4):
                    g = half * 4 + gg
                    R.op("dve", lambda e, g=g, gg=gg, p=p: e.scalar_tensor_tensor(
                        out=gb[:, g * 128:(g + 1) * 128], in0=p[:, gg * 128:(gg + 1) * 128], scalar=bsT[:, g:g + 1],
                        in1=u[:, g * 128:(g + 1) * 128], op0=ALU.add, op1=ALU.mult), rd=[bp, b_bsT, b_u], wr=[b_gb])
            transpose_tile(gb, b_gb, gT, b_gT)
            outs = proj_tm(gT, b_gT, w_out, b_wout, D)
            residual_out(W, outs, xt, bxt, q)
        R.barrier()

    def phase_ffn(l):
        load_mod(l, 1)
        wv = Carver(0, WREG)
        w_up, b_wup = wv.get([128, 8, 2 * FF], BF16, "w_up")
        w_dn, b_wdn = wv.get([128, 22, D], BF16, "w_dn")
        cv = Carver(WREG, WREG + 44 * 1024)
        W = std_work(cv)
        wc, b_wc = cv.get([128, 3, 44], F32, "wconv")
        bc, b_bc = cv.get([128, 44], F32, "bconv")
        hh, b_hh = cv.get([2, D], BF16, "hh")
        hX = [cv.get([128, 8, 130], BF16, "hX%d" % i) for i in range(2)]
        actT, b_actT = cv.get([128, 22, 128], BF16, "actT")
        ca, b_ca = cv.get([128, 128], F32, "ca")
        cg, b_cg = cv.get([128, 128], F32, "cg")
        sg, b_sg = cv.get([128, 128], F32, "sg")
        for a in range(8):
            load_w(w_up[:, a, :], b_wup, f_w_up[l, a * 128:(a + 1) * 128, :])
        for jj in range(22):
            load_w(w_dn[:, jj, :], b_wdn, f_w_down[l, jj * 128:(jj + 1) * 128, :])
        R.dma("sp", wc, f_w_convT[l], wr=[b_wc], sb=b_wc)
        R.dma("sp", bc, f_b_convT[l], wr=[b_bc], sb=b_bc)
        R.op("dve", lambda e: e.memset(hh, 0.0), wr=[b_hh])
        R.dma("sp", cpF[0:2, :], hh, rd=[b_hh], sb=b_hh)
        R.dma("sp", cpF[2 + 2 * NCORE:4 + 2 * NCORE, :], hh, rd=[b_hh], sb=b_hh)
        for q in range(NT):
            xt, bxt, hb, bhb = norm_tile(W, xs[q * 128:(q + 1) * 128, :], q)
            R.dma("sp", hbuf[1 + q * 128:1 + (q + 1) * 128, :], hb, rd=[bhb], sb=bhb)
            if q == 0:
                R.dma("sp", agF_in[0:1, :], hb[0:1, :], rd=[bhb], sb=bhb)
            if q == NT - 1:
                R.dma("sp", agF_in[1:2, :], hb[127:128, :], rd=[bhb], sb=bhb)
        R.barrier()
        b_ag = Buf("agF")
        R.allgather(agF_in, agF_out, sb=b_ag)
        R.barrier()
        R.dma("sp", cpF[2:2 + 2 * NCORE, :], agF_out, sb=b_ag)
        R.barrier()
        R.dma("sp", hh[0:2, :], cpF[bass.ds(pid * 2 + 1, 4), :][0:4:3, :], wr=[b_hh], sb=b_hh)
        R.barrier()
        R.op("dve", lambda e: e.tensor_scalar(out=hh, in0=hh, scalar1=m2t[:, 0:1], scalar2=None, op0=ALU.mult),
             rd=[b_hh, b_m2], wr=[b_hh])
        R.dma("sp", hbuf[0:1, :], hh[0:1, :], rd=[b_hh], sb=b_hh)
        R.dma("sp", hbuf[TPC + 1:TPC + 2, :], hh[1:2, :], rd=[b_hh], sb=b_hh)
        R.barrier()
        for q in range(NT):
            hm, bhm = W.hb[q % 2]
            R.dma("sp", hm, hbuf[1 + q * 128:1 + (q + 1) * 128, :], wr=[bhm], sb=bhm)
            R.dma("sp", hh[0:1, :], hbuf[q * 128:q * 128 + 1, :], wr=[b_hh], sb=b_hh)
            R.dma("sp", hh[1:2, :], hbuf[q * 128 + 129:q * 128 + 130, :], wr=[b_hh], sb=b_hh)
            hx, bhx = hX[q % 2]
            transpose_tile(hm, bhm, hx, bhx, dcol=hx[:, :, 1:129])
            p, bp = next_pT()
            for a in range(8):
                R.op("pe", lambda e, a=a: e.transpose(p[:, a * 128:a * 128 + 2], hh[0:2, a * 128:(a + 1) * 128], ident[0:2, 0:2]),
                     rd=[b_hh, b_ident], wr=[bp], sig=(a == 7))
            pv = p.rearrange("p (a t) -> p a t", a=8)
            R.op("act", lambda e: e.activation(out=hx[:, :, 0:1], in_=pv[:, :, 0:1], func=AF.Copy), rd=[bp], wr=[bhx])
            R.op("act", lambda e: e.activation(out=hx[:, :, 129:130], in_=pv[:, :, 1:2], func=AF.Copy), rd=[bp], wr=[bhx])
            xt, bxt = W.xt[q % 2]
            R.dma("sp", xt, xs[q * 128:(q + 1) * 128, :], wr=[bxt], sb=bxt)
            for jj in range(22):
                res = []
                for part in range(2):
                    ch = jj + 22 * part
                    p, bp = next_pM()
                    for a in range(8):
                        R.op("pe", lambda e, a=a, ch=ch, p=p: e.matmul(p[:, 0:130], lhsT=w_up[:, a, ch * 128:(ch + 1) * 128],
                                                                      rhs=hx[:, a, :], start=(a == 0), stop=(a == 7)),
                             rd=[b_wup, bhx], wr=[bp], sig=(a == 7))
                    dst, bd = (ca, b_ca) if part == 0 else (cg, b_cg)
                    R.op("act", lambda e, p=p, ch=ch, dst=dst: e.activation(out=dst, in_=p[:, 1:129], func=AF.Identity,
                                                                          bias=bc[:, ch:ch + 1], scale=wc[:, 1, ch:ch + 1]),
                         rd=[bp, b_wc, b_bc], wr=[bd])
                    R.op("dve", lambda e, p=p, ch=ch, dst=dst: e.scalar_tensor_tensor(
                        out=dst, in0=p[:, 0:128], scalar=wc[:, 0, ch:ch + 1], in1=dst, op0=ALU.mult, op1=ALU.add),
                        rd=[bp, b_wc, bd], wr=[bd])
                    R.op("dve", lambda e, p=p, ch=ch, dst=dst: e.scalar_tensor_tensor(
                        out=dst, in0=p[:, 2:130], scalar=wc[:, 2, ch:ch + 1], in1=dst, op0=ALU.mult, op1=ALU.add),
                        rd=[bp, b_wc, bd], wr=[bd])
                R.op("act", lambda e: e.activation(out=sg, in_=cg, func=AF.Silu), rd=[b_cg], wr=[b_sg])
                R.op("pool", lambda e, jj=jj: e.tensor_tensor(out=actT[:, jj, :], in0=ca, in1=sg, op=ALU.mult),
                     rd=[b_ca, b_sg], wr=[b_actT])
            outs = proj_tm(actT, b_actT, w_dn, b_wdn, D, K=22)
            residual_out(W, outs, xt, bxt, q)
        R.barrier()


    def phase_attn(l):
        load_mod(l, 0)
        wv = Carver(0, WREG)
        w_qkv, b_wq = wv.get([128, 8, 1536], BF16, "w_qkv")
        w_o, b_wo = wv.get([128, 8, D], BF16, "w_o")
        bias, b_bias = wv.get([128, 16, 384], F32, "bias")
        dist, b_dist = wv.get([128, 384], F32, "dist")
        amk, b_amk = wv.get([128, 384], F32, "amk")
        snk, b_snk = wv.get([128, 16], F32, "snk")
        kvt = [wv.get([128, 512], BF16, "kvt%d" % i) for i in range(2)]
        kh, b_kh = wv.get([128, 2, 512], BF16, "kh")
        zt, b_zt = wv.get([128, 512], BF16, "zt")
        kb = [wv.get([128, 3, 512], BF16, "kb%d" % i) for i in range(2)]
        kT, b_kT = wv.get([128, 2, 384], BF16, "kT")
        qb, b_qb = wv.get([128, D], BF16, "qb")
        qT, b_qT = wv.get([128, 8, 128], BF16, "qT")
        sc = [wv.get([128, 384], F32, "sc%d" % i) for i in range(2)]
        eb = [wv.get([128, 384], BF16, "eb%d" % i) for i in range(2)]
        ebT = [wv.get([128, 3, 128], BF16, "ebT%d" % i) for i in range(2)]
        st = [wv.get([128, 8], F32, "st%d" % i) for i in range(2)]
        ob, b_ob = wv.get([128, D], BF16, "ob")
        oT, b_oT = wv.get([128, 8, 128], BF16, "oT")
        for a in range(8):
            load_w(w_qkv[:, a, :], b_wq, b_w_qkv[a * 128:(a + 1) * 128, :])
            load_w(w_o[:, a, :], b_wo, b_w_o[a * 128:(a + 1) * 128, :])
        R.dma("sp", dist, adist, wr=[b_dist], sb=b_dist)
        R.dma("sp", amk, amask, wr=[b_amk], sb=b_amk)
        R.dma("sp", snk, b_sinks[0:1, :].partition_broadcast(128), wr=[b_snk], sb=b_snk)
        for h in range(16):
            slope = float(2.0 ** (-8.0 * (h + 1) / 16))
            R.op("dve", lambda e, h=h, slope=slope: e.scalar_tensor_tensor(out=bias[:, h, :], in0=dist, scalar=-slope, in1=amk,
                                                                           op0=ALU.mult, op1=ALU.add),
                 rd=[b_dist, b_amk], wr=[b_bias])
        W = std_work(Carver(WREG, WREG + 44 * 1024))
        R.op("dve", lambda e: e.memset(zt, 0.0), wr=[b_zt])
        for blk in (0, NCORE + 1):
            for hf in range(2):
                R.dma("sp", cpK[blk * 256 + hf * 128:blk * 256 + (hf + 1) * 128, :], zt, rd=[b_zt], sb=b_zt)
        for q in range(NT):
            xt, bxt, hb, bhb = norm_tile(W, xs[q * 128:(q + 1) * 128, :], q)
            hT, bhT = W.hT[q % 2]
            transpose_tile(hb, bhb, hT, bhT)
            p, bp = next_pM()
            for a in range(8):
                R.op("pe", lambda e, a=a, p=p: e.matmul(p, lhsT=hT[:, a, :], rhs=w_qkv[:, a, 1024:1536], start=(a == 0), stop=(a == 7)),
                     rd=[bhT, b_wq], wr=[bp], sig=(a == 7))
            kv, bkv = kvt[q % 2]
            R.op("act", lambda e, p=p, kv=kv: e.activation(out=kv, in_=p, func=AF.Copy), rd=[bp], wr=[bkv])
            R.dma("sp", kvbuf[128 + q * 128:128 + (q + 1) * 128, :], kv, rd=[bkv], sb=bkv)
            if q == 0:
                R.dma("sp", agK_in[0:128, :], kv, rd=[bkv], sb=bkv)
            if q == NT - 1:
                R.dma("sp", agK_in[128:256, :], kv, rd=[bkv], sb=bkv)
        R.barrier()
        b_ag = Buf("agK")
        R.allgather(agK_in, agK_out, sb=b_ag)
        R.barrier()
        R.dma("sp", cpK[256:256 + 256 * NCORE, :], agK_out, sb=b_ag)
        R.barrier()
        R.dma("sp", kh, cpK.rearrange("(b p) c -> p b c", p=128)[:, bass.ds(pid * 2 + 1, 4), :][:, 0:4:3, :], wr=[b_kh], sb=b_kh)
        for sd in range(2):
            R.op("dve", lambda e, sd=sd: e.tensor_scalar(out=kh[:, sd, :], in0=kh[:, sd, :], scalar1=mmt[:, sd:sd + 1], scalar2=None,
                                                         op0=ALU.mult), rd=[b_kh, b_mm], wr=[b_kh])
        R.dma("sp", kvbuf[0:128, :], kh[:, 0, :], rd=[b_kh], sb=b_kh)
        R.dma("sp", kvbuf[TPC + 128:TPC + 256, :], kh[:, 1, :], rd=[b_kh], sb=b_kh)
        R.barrier()
        for q in range(NT):
            xt, bxt, hb, bhb = norm_tile(W, xs[q * 128:(q + 1) * 128, :], q)
            hT, bhT = W.hT[q % 2]
            transpose_tile(hb, bhb, hT, bhT)
            kbt, bkb = kb[q % 2]
            R.dma("sp", kbt, kvbuf[q * 128:q * 128 + 384, :].rearrange("(b p) c -> p b c", p=128), wr=[bkb], sb=bkb)
            for g in range(2):
                p, bp = next_pM()
                for a in range(8):
                    R.op("pe", lambda e, a=a, p=p, g=g: e.matmul(p, lhsT=hT[:, a, :], rhs=w_qkv[:, a, g * 512:(g + 1) * 512],
                                                                start=(a == 0), stop=(a == 7)),
                         rd=[bhT, b_wq], wr=[bp], sig=(a == 7))
                R.op("act", lambda e, p=p, g=g: e.activation(
                    out=qb[:, g * 512:(g + 1) * 512].rearrange("p (i k d) -> p k i d", i=4, k=2, d=64),
                    in_=p.rearrange("p (k i d) -> p k i d", k=2, i=4, d=64), func=AF.Copy), rd=[bp], wr=[b_qb])
            transpose_tile(qb, b_qb, qT, b_qT)
            p, bp = next_pT()
            for c2 in range(2):
                for b in range(3):
                    R.op("pe", lambda e, c2=c2, b=b, p=p: e.transpose(p[:, (c2 * 3 + b) * 128:(c2 * 3 + b + 1) * 128],
                                                                     kbt[:, b, c2 * 128:(c2 + 1) * 128], ident),
                         rd=[bkb, b_ident], wr=[bp], sig=(c2 == 1 and b == 2))
            R.op("act", lambda e, p=p: e.activation(out=kT.rearrange("p c k -> p (c k)"), in_=p[:, 0:768], func=AF.Copy),
                 rd=[bp], wr=[b_kT])
            po = [next_pM(), next_pM()]
            for h in range(16):
                g, k, i = h // 8, (h // 4) % 2, h % 4
                kvh = h // 4
                s_, bs_ = sc[h % 2]
                e_, be_ = eb[h % 2]
                eT, beT = ebT[h % 2]
                t_, bt_ = st[h % 2]
                p, bp = next_pM()
                R.op("pe", lambda e, p=p, g=g, k=k, i=i: e.matmul(p[:, 0:384], lhsT=qT[k * 64:(k + 1) * 64, g * 4 + i, :],
                                                                 rhs=kT[k * 64:(k + 1) * 64, g, :], start=True, stop=True),
                     rd=[b_qT, b_kT], wr=[bp])
                R.op("dve", lambda e, p=p, h=h, s_=s_: e.scalar_tensor_tensor(out=s_, in0=p[:, 0:384], scalar=0.125, in1=bias[:, h, :],
                                                                              op0=ALU.mult, op1=ALU.add),
                     rd=[bp, b_bias], wr=[bs_])
                if q == 0:
                    R.op("dve", lambda e, s_=s_: e.tensor_scalar(out=s_[:, 0:128], in0=s_[:, 0:128], scalar1=mkt[:, 0:1], scalar2=None,
                                                                 op0=ALU.add), rd=[bs_, b_mk], wr=[bs_])
                if q == NT - 1:
                    R.op("dve", lambda e, s_=s_: e.tensor_scalar(out=s_[:, 256:384], in0=s_[:, 256:384], scalar1=mkt[:, 1:2],
                                                                 scalar2=None, op0=ALU.add), rd=[bs_, b_mk], wr=[bs_])
                R.op("dve", lambda e, s_=s_, t_=t_: e.reduce_max(out=t_[:, 0:1], in_=s_, axis=AX.X), rd=[bs_], wr=[bt_])
                R.op("dve", lambda e, t_=t_, h=h: e.tensor_tensor(out=t_[:, 1:2], in0=t_[:, 0:1], in1=snk[:, h:h + 1], op=ALU.max),
                     rd=[bt_, b_snk], wr=[bt_])
                R.op("dve", lambda e, t_=t_: e.tensor_scalar(out=t_[:, 2:3], in0=t_[:, 1:2], scalar1=-1.0, scalar2=None, op0=ALU.mult),
                     rd=[bt_], wr=[bt_])
                R.op("act", lambda e, s_=s_, e_=e_, t_=t_: e.activation(out=e_, in_=s_, func=AF.Exp, bias=t_[:, 2:3], scale=1.0,
                                                                        accum_out=t_[:, 3:4]), rd=[bs_, bt_], wr=[be_, bt_])
                R.op("act", lambda e, t_=t_, h=h: e.activation(out=t_[:, 4:5], in_=snk[:, h:h + 1], func=AF.Exp, bias=t_[:, 2:3],
                                                               scale=1.0), rd=[b_snk, bt_], wr=[bt_])
                R.op("dve", lambda e, t_=t_: e.tensor_tensor(out=t_[:, 5:6], in0=t_[:, 3:4], in1=t_[:, 4:5], op=ALU.add),
                     rd=[bt_], wr=[bt_])
                R.op("dve", lambda e, t_=t_: e.reciprocal(out=t_[:, 6:7], in_=t_[:, 5:6]), rd=[bt_], wr=[bt_])
                pt, bpt = next_pT()
                for b in range(3):
                    R.op("pe", lambda e, b=b, pt=pt, e_=e_: e.transpose(pt[:, b * 128:(b + 1) * 128], e_[:, b * 128:(b + 1) * 128], ident),
                         rd=[be_, b_ident], wr=[bpt], sig=(b == 2))
                R.op("act", lambda e, pt=pt, eT=eT: e.activation(out=eT.rearrange("p b k -> p (b k)"), in_=pt[:, 0:384], func=AF.Copy),
                     rd=[bpt], wr=[beT])
                pp, bpp = po[h // 8]
                sl = slice((h % 8) * 64, (h % 8 + 1) * 64)
                for b in range(3):
                    R.op("pe", lambda e, b=b, pp=pp, sl=sl, eT=eT, kvh=kvh: e.matmul(pp[:, sl], lhsT=eT[:, b, :],
                                                                                     rhs=kbt[:, b, 256 + kvh * 64:256 + (kvh + 1) * 64],
                                                                                     start=(b == 0), stop=(b == 2)),
                         rd=[beT, bkb], wr=[bpp], sig=(b == 2))
                R.op("dve", lambda e, pp=pp, sl=sl, h=h, t_=t_: e.tensor_scalar(out=ob[:, h * 64:(h + 1) * 64], in0=pp[:, sl],
                                                                                scalar1=t_[:, 6:7], scalar2=None, op0=ALU.mult),
                     rd=[bpp, bt_], wr=[b_ob])
            transpose_tile(ob, b_ob, oT, b_oT)
            outs = proj_tm(oT, b_oT, w_o, b_wo, D)
            residual_out(W, outs, xt, bxt, q)
        R.barrier()

    def phase_fft(l):
        load_mod(l, 0)
        W = std_work(Carver(WREG, WREG + 44 * 1024))
        for q in range(NT):
            xt, bxt, hb, bhb = norm_tile(W, xs[q * 128:(q + 1) * 128, :], q)
            hT, bhT = W.hT[q % 2]
            transpose_tile(hb, bhb, hT, bhT)
            R.dma("sp", agH_in[q * 128:(q + 1) * 128, :], hT.rearrange("p a t -> p (a t)"), rd=[bhT], sb=bhT)
        R.barrier()
        b_ag = Buf("agH")
        R.allgather(agH_in, agH_out, sb=b_ag)
        R.barrier()
        wv = Carver(0, WREG)
        wgT, b_wgT = wv.get([128, D], BF16, "wgT")
        csm, b_csm = wv.get([128, 256], BF16, "csm")
        wz, b_wz = wv.get([128, 8, 256], BF16, "wz")
        hTt = [wv.get([128, 8, 128], BF16, "hTt%d" % i) for i in range(3)]
        ztl = [wv.get([128, 256], BF16, "ztl%d" % i) for i in range(3)]
        load_w(wgT, b_wgT, c_w_inT)
        R.dma("sp", csm, cs128, wr=[b_csm], sb=b_csm)
        for a in range(8):
            p, bp = next_pM()
            R.op("pe", lambda e, a=a, p=p: e.matmul(p[:, 0:256], lhsT=wgT[:, a * 128:(a + 1) * 128], rhs=csm, start=True, stop=True),
                 rd=[b_wgT, b_csm], wr=[bp])
            R.op("act", lambda e, a=a, p=p: e.activation(out=wz[:, a, :], in_=p[:, 0:256], func=AF.Copy), rd=[bp], wr=[b_wz])
        for gq in range(NTOK // 128):
            ht, bht = hTt[gq % 3]
            R.dma("sp", ht.rearrange("p a t -> p (a t)"), agH_out[gq * 128:(gq + 1) * 128, :], wr=[bht], sb=bht)
            p, bp = next_pM()
            for a in range(8):
                R.op("pe", lambda e, a=a, p=p, ht=ht: e.matmul(p[:, 0:256], lhsT=ht[:, a, :], rhs=wz[:, a, :], start=(a == 0), stop=(a == 7)),
                     rd=[bht, b_wz], wr=[bp], sig=(a == 7))
            z_, bz_ = ztl[gq % 3]
            R.op("act", lambda e, p=p, z_=z_: e.activation(out=z_, in_=p[:, 0:256], func=AF.Copy), rd=[bp], wr=[bz_])
            R.dma("sp", zbuf[gq * 128:(gq + 1) * 128, :], z_, rd=[bz_], sb=bz_)
        R.barrier()
        for (st0, S) in cfg.seqs:
            N1 = S // 128
            t2c_d, t2s_d, t1a_d, t1b_d = tabs[S]
            fv = Carver(0, WREG + 44 * 1024)
            zz, b_zz = fv.get([128, 128, 256], BF16, "zz")
            U, b_U = fv.get([128, N1, 2, 128], BF16, "U")
            t1a, b_t1a = fv.get([128, 2 * N1], BF16, "t1a")
            t1b, b_t1b = fv.get([128, 2 * N1], BF16, "t1b")
            KC = min(16, N1)
            tc_ = [fv.get([128, KC * 128], BF16, "tc%d" % i) for i in range(2)]
            ts_ = [fv.get([128, KC * 128], BF16, "ts%d" % i) for i in range(2)]
            ys = [fv.get([128, KC, 128], BF16, "ys%d" % i) for i in range(2)]
            R.dma("sp", zz[0:N1].rearrange("p s c -> p (s c)"),
                  zbuf[st0:st0 + S, :].rearrange("(s1 s2) c -> s1 (s2 c)", s2=128), wr=[b_zz], sb=b_zz)
            R.dma("sp", t1a[0:N1, :], t1a_d, wr=[b_t1a], sb=b_t1a)
            R.dma("sp", t1b[0:N1, :], t1b_d, wr=[b_t1b], sb=b_t1b)
            cpb = 512 // (2 * N1)
            for c0 in range(0, 128, cpb):
                p, bp = next_pM()
                for ci in range(cpb):
                    c = c0 + ci
                    o_ = p[:, ci * 2 * N1:(ci + 1) * 2 * N1]
                    R.op("pe", lambda e, o_=o_, c=c: e.matmul(o_, lhsT=zz[0:N1, :, c], rhs=t1a[0:N1, :], start=True, stop=False),
                         rd=[b_zz, b_t1a], wr=[bp], sig=False)
                    R.op("pe", lambda e, o_=o_, c=c: e.matmul(o_, lhsT=zz[0:N1, :, 128 + c], rhs=t1b[0:N1, :], start=False, stop=True),
                         rd=[b_zz, b_t1b], wr=[bp], sig=(ci == cpb - 1))
                R.op("act", lambda e, p=p, c0=c0: e.activation(
                    out=U[:, :, :, c0:c0 + cpb].rearrange("p k r c -> p c r k"),
                    in_=p[:, 0:cpb * 2 * N1].rearrange("p (c r k) -> p c r k", c=cpb, r=2), func=AF.Copy), rd=[bp], wr=[b_U])
            scale = float(1.0 / np.sqrt(S * 128.0))
            ydst = agY_in[st0:st0 + S, :].rearrange("(k2 k1) c -> k2 k1 c", k1=N1)
            for kc in range(0, N1, KC):
                tcc, btc = tc_[(kc // KC) % 2]
                tss, bts = ts_[(kc // KC) % 2]
                yy, byy = ys[(kc // KC) % 2]
                R.dma("sp", tcc, t2c_d[:, kc * 128:(kc + KC) * 128], wr=[btc], sb=btc)
                R.dma("sp", tss, t2s_d[:, kc * 128:(kc + KC) * 128], wr=[bts], sb=bts)
                for k4 in range(0, KC, 4):
                    p, bp = next_pM()
                    nk = min(4, KC - k4)
                    for kk in range(nk):
                        k1 = kc + k4 + kk
                        o_ = p[:, kk * 128:(kk + 1) * 128]
                        R.op("pe", lambda e, o_=o_, k1=k1, kk=kk, k4=k4, tcc=tcc: e.matmul(
                            o_, lhsT=tcc[:, (k4 + kk) * 128:(k4 + kk + 1) * 128], rhs=U[:, k1, 0, :], start=True, stop=False),
                            rd=[btc, b_U], wr=[bp], sig=False)
                        R.op("pe", lambda e, o_=o_, k1=k1, kk=kk, k4=k4, tss=tss: e.matmul(
                            o_, lhsT=tss[:, (k4 + kk) * 128:(k4 + kk + 1) * 128], rhs=U[:, k1, 1, :], start=False, stop=True),
                            rd=[bts, b_U], wr=[bp], sig=(kk == nk - 1))
                    R.op("act", lambda e, p=p, k4=k4, nk=nk, yy=yy: e.activation(
                        out=yy[:, k4:k4 + nk, :].rearrange("p k c -> p (k c)"), in_=p[:, 0:nk * 128], func=AF.Copy, scale=scale),
                        rd=[bp], wr=[byy])
                R.dma("sp", ydst[:, kc:kc + KC, :], yy, rd=[byy], sb=byy)
            R.barrier()
        b_agy = Buf("agY")
        R.allgather(agY_in, agY_out, sb=b_agy)
        R.barrier()
        nrow = NCORE * NTOK
        step = nrow // 8
        for i in range(8):
            R.dma("sp", cpY[i * step:(i + 1) * step, :], agY_out[i * step:(i + 1) * step, :], sb=b_agy)
        R.barrier()
        wv = Carver(0, WREG)
        w_out, b_wout = wv.get([128, 8, D], BF16, "c_w_out")
        ft = [wv.get([128, 8, 128], BF16, "ft%d" % i) for i in range(2)]
        fT, b_fT = wv.get([128, 8, 128], BF16, "fT")
        for a in range(8):
            load_w(w_out[:, a, :], b_wout, c_w_out[a * 128:(a + 1) * 128, :])
        cpYv = cpY.rearrange("(g t) c -> g t c", g=NCORE)
        b_fown = Buf("fown")
        for g in range(NCORE):
            R.dma("act", fown[g], cpYv[g, bass.ds(pid_act * TPC, TPC), :], sb=b_fown)
        R.barrier()
        for q in range(NT):
            f_, bf_ = ft[q % 2]
            R.dma("sp", f_, fown[:, q * 128:(q + 1) * 128, :].rearrange("g t c -> t g c"), wr=[bf_], sb=bf_)
            xt, bxt = W.xt[q % 2]
            R.dma("sp", xt, xs[q * 128:(q + 1) * 128, :], wr=[bxt], sb=bxt)
            transpose_tile(f_.rearrange("p g c -> p (g c)"), bf_, fT, b_fT)
            outs = proj_tm(fT, b_fT, w_out, b_wout, D)
            residual_out(W, outs, xt, bxt, q)
        R.barrier()

    def final_phase():
        wkk = Carver(WREG, WREG + 44 * 1024)
        W = std_work(wkk)
        gf, b_gf = Carver(0, WREG).get([128, D], F32, "gf")
        R.dma("sp", gf, g_final[0:1, :].partition_broadcast(128), wr=[b_gf], sb=b_gf)
        for q in range(NT):
            xt, bxt = W.xt[q % 2]
            R.dma("sp", xt, xs[q * 128:(q + 1) * 128, :], wr=[bxt], sb=bxt)
            junk, bj = W.junk
            ss, bss = W.ss[q % 2]
            R.op("act", lambda e: e.activation(out=junk, in_=xt, func=AF.Square, accum_out=ss[:, 0:1]), rd=[bxt], wr=[bj, bss])
            R.op("dve", lambda e: e.tensor_scalar(out=ss[:, 1:2], in0=ss[:, 0:1], scalar1=1.0 / D, scalar2=EPS, op0=ALU.mult,
                                                  op1=ALU.add), rd=[bss], wr=[bss])
            R.op("act", lambda e: e.activation(out=ss[:, 2:3], in_=ss[:, 1:2], func=AF.Sqrt), rd=[bss], wr=[bss])
            R.op("dve", lambda e: e.reciprocal(out=ss[:, 3:4], in_=ss[:, 2:3]), rd=[bss], wr=[bss])
            R.op("dve", lambda e: e.scalar_tensor_tensor(out=xt, in0=xt, scalar=ss[:, 3:4], in1=gf, op0=ALU.mult, op1=ALU.mult),
                 rd=[bxt, bss, b_gf], wr=[bxt])
            R.dma("sp", y_out[q * 128:(q + 1) * 128, :], xt, rd=[bxt], sb=bxt)
        R.barrier()

    stages = debug or "all"
    plan = {"g0": ["g0"], "g0f0": ["g0", "f0"], "l1m": ["g0", "f0", "a1"], "l1": ["g0", "f0", "a1", "f1"],
            "l2m": ["g0", "f0", "a1", "f1", "c2"], "l2": ["g0", "f0", "a1", "f1", "c2", "f2"],
            "all": ["g0", "f0", "a1", "f1", "c2", "f2", "g3", "f3"]}[stages]
    for ph in plan:
        if ph[0] == "g":
            phase_gmlp(int(ph[1]), int(ph[1]) // 3)
        elif ph[0] == "f":
            phase_ffn(int(ph[1]))
        elif ph[0] == "a":
            phase_attn(int(ph[1]))
        elif ph[0] == "c":
            phase_fft(int(ph[1]))
    if stages != "all":
        W = std_work(Carver(WREG, WREG + 44 * 1024))
        for q in range(NT):
            xt, bxt = W.xt[q % 2]
            R.dma("sp", xt, xs[q * 128:(q + 1) * 128, :], wr=[bxt], sb=bxt)
            R.dma("sp", dbg_out[q * 128:(q + 1) * 128, :], xt, rd=[bxt], sb=bxt)
        R.barrier()
    final_phase()
    return nc


def core_inputs(cfg, r, inp, x_all, c_all):
    TPC = cfg.TPC
    seq = 0 if r < 2 else (1 if r < 4 else 2)
    first = r in (0, 2, 4)
    last = r in (1, 3, 7)
    c = c_all[seq]
    m = {}
    m["x"] = np.ascontiguousarray(x_all[r * TPC:(r + 1) * TPC])
    m["cT"] = np.ascontiguousarray(c.reshape(8, 128).T)
    for kname in ("w_ada", "b_ada", "norm_g", "a_w_in", "a_g_v", "a_w_out", "c_w_out", "f_w_up", "f_w_down"):
        m[kname] = inp[kname]
    m["a_w_sT"] = np.ascontiguousarray(np.transpose(inp["a_w_s"], (0, 1, 3, 2)))
    m["a_b_sT"] = np.ascontiguousarray(np.transpose(inp["a_b_s"], (0, 2, 1)))
    m["b_w_qkv"] = inp["b_w_qkv"][0]
    m["b_sinks"] = inp["b_sinks"]
    m["b_w_o"] = inp["b_w_o"][0]
    m["c_w_inT"] = np.ascontiguousarray(inp["c_w_in"][0][:, r * 128:(r + 1) * 128].T)
    m["c_w_out"] = inp["c_w_out"][0]
    m["f_w_convT"] = np.ascontiguousarray(np.transpose(inp["f_w_conv"].reshape(4, 3, 44, 128), (0, 3, 1, 2)))
    m["f_b_convT"] = np.ascontiguousarray(np.transpose(inp["f_b_conv"].reshape(4, 44, 128), (0, 2, 1)))
    m["g_final"] = inp["g_final"].reshape(1, D)
    m["ident"] = bf(np.eye(128))
    m["m2"] = np.array([[0.0 if first else 1.0], [0.0 if last else 1.0]], np.float32)
    mk = np.zeros((128, 2), np.float32)
    mk[:, 0] = NEG if first else 0.0
    mk[:, 1] = NEG if last else 0.0
    m["mk"] = mk
    mm = np.ones((128, 2), np.float32)
    mm[:, 0] = 0.0 if first else 1.0
    mm[:, 1] = 0.0 if last else 1.0
    m["mm"] = mm
    qp = np.arange(128)[:, None]
    kp = np.arange(384)[None, :] - 128
    dist = np.abs(qp - kp).astype(np.float32)
    m["adist"] = dist
    m["amask"] = np.where(dist <= 128, 0.0, NEG).astype(np.float32)
    cc = np.arange(128, dtype=np.float64)
    ph = 2 * np.pi * ((cc[:, None] * cc[None, :]) % 128) / 128
    m["cs128"] = bf(np.concatenate([np.cos(ph), np.sin(ph)], axis=1))
    done = set()
    for (st, S) in cfg.seqs:
        if S in done:
            continue
        done.add(S)
        t2c, t2s, t1a, t1b = dft_tables(S)
        m["t2c_%d" % S], m["t2s_%d" % S], m["t1a_%d" % S], m["t1b_%d" % S] = t2c, t2s, t1a, t1b
    return m


def run(cfg, inp, x_all, c_all, debug=None):
    nc = build(cfg, debug)
    in_maps = [core_inputs(cfg, r, inp, x_all, c_all) for r in range(NCORE)]
    res = run_bass_kernel_spmd(nc, in_maps, core_ids=list(range(NCORE)))
    y = np.concatenate([res.results[r]["y"] for r in range(NCORE)], axis=0)
    dbg = None
    if debug:
        dbg = np.concatenate([res.results[r]["dbg"] for r in range(NCORE)], axis=0)
    return y, dbg


def kernel(**inputs):
    inp = {k: np.asarray(v) for k, v in inputs.items()}
    xp, xsm = inp["x_prompt"], inp["x_sample"]
    B, S, _ = xp.shape
    tpc = S // 2
    cfg = Cfg(tpc)
    x_all = np.concatenate([xp.reshape(B * S, D), xsm.reshape(-1, D)], axis=0)
    c_all = [inp["c_prompt"][0], inp["c_prompt"][1], inp["c_sample"][0]]
    y, _ = run(cfg, inp, x_all, c_all)
    y_prompt = y[:B * S].reshape(B, S, D).astype(np.float32)
    y_sample = y[B * S:].reshape(xsm.shape).astype(np.float32)
    return (y_prompt, y_sample)
```

```python
import numpy as np
import ml_dtypes
import concourse.bass as bass
import concourse.mybir as mybir
from concourse.bass_utils import run_bass_kernel_spmd

F32 = mybir.dt.float32
BF16 = mybir.dt.bfloat16
I32 = mybir.dt.int32
AF = mybir.ActivationFunctionType
ALU = mybir.AluOpType
AX = mybir.AxisListType

D = 1024
FF = 2816
NCORE = 8
EPS = 1e-6
NEG = -1e30


class Tok:
    __slots__ = ("sem", "val", "eng", "buf")

    def __init__(self, sem, val, eng, buf=None):
        self.sem, self.val, self.eng, self.buf = sem, val, eng, buf


class Buf:
    def __init__(self, name):
        self.name = name
        self.lw = None
        self.rd = []
        self.dsem = None
        self.dcnt = 0


class Rec:
    def __init__(self, nc):
        self.nc = nc
        self.E = {"pe": nc.tensor, "dve": nc.vector, "act": nc.scalar, "pool": nc.gpsimd, "sp": nc.sync}
        self.sem = {e: nc.alloc_semaphore("s_" + e) for e in self.E}
        self.cnt = {e: 0 for e in self.E}
        self.seen = {e: {} for e in self.E}
        self.pending = {e: [] for e in self.E}
        self.dbufs = []
        self.nsem = 0
        self.pools = {"d": [], "g": [], "c": []}

    def _getsem(self, sb, prefix):
        sb.kind = prefix
        if self.pools[prefix]:
            sb.dsem, sb.dcnt = self.pools[prefix].pop()
        else:
            sb.dsem = self.nc.alloc_semaphore("%s_%d" % (prefix, self.nsem))
            self.nsem += 1
            sb.dcnt = 0
        self.dbufs.append(sb)

    def _wait(self, eng, t):
        if t is None:
            return
        if t.buf is not None:
            if t.buf.dsem is not t.sem:
                return
            val = t.buf.dcnt
        else:
            if t.eng == eng and eng == "pe":
                return
            val = t.val
            assert val is not None, "dependency on un-signalled instruction"
        sid = id(t.sem)
        if self.seen[eng].get(sid, 0) >= val:
            return
        self.E[eng].wait_ge(t.sem, val)
        self.seen[eng][sid] = val

    def _deps(self, eng, rd, wr):
        for b in rd:
            self._wait(eng, b.lw)
        for b in wr:
            self._wait(eng, b.lw)
            for t in b.rd:
                self._wait(eng, t)

    def _upd(self, tok, rd, wr):
        for b in wr:
            b.lw = tok
            b.rd = []
        for b in rd:
            b.rd = [t for t in b.rd if t.sem is not tok.sem] + [tok]

    def op(self, eng, fn, rd=(), wr=(), sig=True):
        self._deps(eng, rd, wr)
        ins = fn(self.E[eng])
        tok = Tok(self.sem[eng], None, eng)
        self.pending[eng].append(tok)
        if sig:
            self.cnt[eng] += 1
            ins.then_inc(self.sem[eng], 1)
            for t in self.pending[eng]:
                t.val = self.cnt[eng]
            self.pending[eng] = []
        self._upd(tok, rd, wr)
        return tok

    def dma(self, eng, out, in_, rd=(), wr=(), sb=None):
        self._deps(eng, rd, wr)
        kind = "g" if eng == "pool" else "d"
        if sb.dsem is None:
            self._getsem(sb, kind)
        assert sb.kind == kind, ("semaphore kind mix", sb.name)
        ins = self.E[eng].dma_start(out=out, in_=in_)
        sb.dcnt += 16
        ins.then_inc(sb.dsem, 16)
        tok = Tok(sb.dsem, sb.dcnt, None, sb)
        self._upd(tok, rd, wr)
        return tok

    def allgather(self, in_ap, out_ap, rd=(), wr=(), sb=None):
        eng = "pool"
        self._deps(eng, rd, wr)
        if sb.dsem is None:
            self._getsem(sb, "c")
        assert sb.kind == "c"
        ins = self.nc.gpsimd.collective_compute(
            "AllGather", ALU.bypass, replica_groups=[list(range(NCORE))], ins=[in_ap.opt()], outs=[out_ap.opt()]
        )
        sb.dcnt += 1
        ins.then_inc(sb.dsem)
        tok = Tok(sb.dsem, sb.dcnt, None, sb)
        self._upd(tok, rd, wr)
        return tok

    def barrier(self):
        for e in self.E:
            assert not self.pending[e], "barrier with un-signalled instructions on " + e
        for e in self.E:
            for o in self.E:
                if o == e:
                    continue
                v = self.cnt[o]
                if v and self.seen[e].get(id(self.sem[o]), 0) < v:
                    self.E[e].wait_ge(self.sem[o], v)
                    self.seen[e][id(self.sem[o])] = v
            for b in self.dbufs:
                if b.dcnt and self.seen[e].get(id(b.dsem), 0) < b.dcnt:
                    self.E[e].wait_ge(b.dsem, b.dcnt)
                    self.seen[e][id(b.dsem)] = b.dcnt
        for b in self.dbufs:
            self.pools[b.kind].append((b.dsem, b.dcnt))
            b.dsem = None
            b.dcnt = 0
        self.dbufs = []


class Cfg:
    def __init__(self, tpc):
        self.TPC = tpc
        self.NT = tpc // 128
        self.NTOK = tpc * NCORE
        self.TF = min(256, tpc)
        self.seqs = [(0, 2 * tpc), (2 * tpc, 2 * tpc), (4 * tpc, 4 * tpc)]


def bf(a):
    return np.ascontiguousarray(np.asarray(a, np.float32)).astype(ml_dtypes.bfloat16)


def dft_tables(S):
    N1 = S // 128
    s2 = np.arange(128, dtype=np.float64)[:, None, None]
    k1 = np.arange(N1, dtype=np.float64)[None, :, None]
    k2 = np.arange(128, dtype=np.float64)[None, None, :]
    ph = 2 * np.pi * ((s2 * (k1 + N1 * k2)) % S) / S
    c2, s2t = np.cos(ph), -np.sin(ph)
    a = np.arange(N1, dtype=np.float64)
    ph1 = 2 * np.pi * ((a[:, None] * a[None, :]) % N1) / N1
    t1a = np.concatenate([np.cos(ph1), np.sin(ph1)], axis=1)
    t1b = np.concatenate([-np.sin(ph1), np.cos(ph1)], axis=1)
    return bf(c2.reshape(128, N1 * 128)), bf(s2t.reshape(128, N1 * 128)), bf(t1a), bf(t1b)


def build(cfg, debug=None):
    TPC, NT, NTOK = cfg.TPC, cfg.NT, cfg.NTOK
    nc = bass.Bass("TRN2", target_bir_lowering=False)
    R = Rec(nc)

    def din(name, shape, dt=F32):
        return nc.dram_tensor(name, list(shape), dt, kind="ExternalInput").ap()

    def dscr(name, shape, dt):
        return nc.dram_tensor(name, list(shape), dt).ap()

    x_in = din("x", [TPC, D])
    cT = din("cT", [128, 8])
    w_ada = din("w_ada", [4, D, 6 * D])
    b_ada = din("b_ada", [4, 6 * D])
    norm_g = din("norm_g", [4, 2, D])
    a_w_in = din("a_w_in", [2, D, 2 * D])
    a_g_v = din("a_g_v", [2, D])
    a_w_sT = din("a_w_sT", [2, 8, 128, 128])
    a_b_sT = din("a_b_sT", [2, 128, 8])
    a_w_out = din("a_w_out", [2, D, D])
    b_w_qkv = din("b_w_qkv", [D, 1536])
    b_sinks = din("b_sinks", [1, 16])
    b_w_o = din("b_w_o", [D, D])
    c_w_inT = din("c_w_inT", [128, D])
    c_w_out = din("c_w_out", [D, D])
    f_w_up = din("f_w_up", [4, D, 2 * FF])
    f_w_convT = din("f_w_convT", [4, 128, 3, 44])
    f_b_convT = din("f_b_convT", [4, 128, 44])
    f_w_down = din("f_w_down", [4, FF, D])
    g_final = din("g_final", [1, D])
    ident_in = din("ident", [128, 128], BF16)
    m2_in = din("m2", [2, 1])
    mk_in = din("mk", [128, 2])
    mm_in = din("mm", [128, 2])
    adist = din("adist", [128, 384])
    amask = din("amask", [128, 384])
    cs128 = din("cs128", [128, 256], BF16)
    tabs = {}
    for (st, S) in cfg.seqs:
        if S not in tabs:
            N1 = S // 128
            tabs[S] = (din("t2c_%d" % S, [128, N1 * 128], BF16), din("t2s_%d" % S, [128, N1 * 128], BF16),
                       din("t1a_%d" % S, [N1, 2 * N1], BF16), din("t1b_%d" % S, [N1, 2 * N1], BF16))
    y_out = nc.dram_tensor("y", [TPC, D], F32, kind="ExternalOutput").ap()
    dbg_out = None
    if debug:
        dbg_out = nc.dram_tensor("dbg", [TPC, D], F32, kind="ExternalOutput").ap()

    xs = dscr("xs", [TPC, D], F32)
    modrow = dscr("modrow", [4, 6 * D], F32)
    hbuf = dscr("hbuf", [TPC + 2, D], BF16)
    agF_in = dscr("agF_in", [2, D], BF16)
    agF_out = dscr("agF_out", [2 * NCORE, D], BF16)
    cpF = dscr("cpF", [2 * NCORE + 4, D], BF16)
    kvbuf = dscr("kvbuf", [TPC + 256, 512], BF16)
    agK_in = dscr("agK_in", [256, 512], BF16)
    agK_out = dscr("agK_out", [256 * NCORE, 512], BF16)
    cpK = dscr("cpK", [256 * (NCORE + 2), 512], BF16)
    agH_in = dscr("agH_in", [TPC, D], BF16)
    agH_out = dscr("agH_out", [NTOK, D], BF16)
    zbuf = dscr("zbuf", [NTOK, 256], BF16)
    agY_in = dscr("agY_in", [NTOK, 128], BF16)
    agY_out = dscr("agY_out", [NCORE * NTOK, 128], BF16)
    cpY = dscr("cpY", [NCORE * NTOK, 128], BF16)

    fown = dscr("fown", [NCORE, TPC, 128], BF16)
    pid = nc.partition_id([mybir.EngineType.SP])
    pid_act = nc.partition_id([mybir.EngineType.Activation])

    ARENA = 198 * 1024
    WORK_END = ARENA - 15872
    arena = nc.alloc_sbuf_tensor("arena", [128, ARENA // 2], BF16)

    class Carver:
        def __init__(self, lo, hi):
            self.lo, self.hi, self.p = lo, hi, lo

        def get(self, shape, dt, name):
            esz = 4 if dt == F32 or dt == I32 else 2
            n = int(np.prod(shape[1:]))
            nb = (n * esz + 31) // 32 * 32
            assert self.p + nb <= self.hi, ("arena overflow", name, self.p, nb, self.hi)
            v = arena[0:shape[0], self.p // 2:(self.p + n * esz) // 2]
            if esz == 4:
                v = v.bitcast(dt)
            self.p += nb
            if len(shape) == 3:
                v = v.rearrange("p (a b) -> p a b", a=shape[1])
            elif len(shape) == 4:
                v = v.rearrange("p (a b c) -> p a b c", a=shape[1], b=shape[2])
            return v, Buf(name)

    WREG = 132 * 1024
    pers = Carver(WORK_END, ARENA)
    ident, b_ident = pers.get([128, 128], BF16, "ident")
    csb, b_csb = pers.get([128, 8, 128], BF16, "csb")
    ones32, b_ones = pers.get([128, 128], F32, "ones")
    modp, b_modp = pers.get([128, 3, D], F32, "modp")
    m2t, b_m2 = pers.get([2, 1], F32, "m2")
    mkt, b_mk = pers.get([128, 2], F32, "mk")
    mmt, b_mm = pers.get([128, 2], F32, "mm")
    small, b_small = pers.get([128, 64], F32, "small")

    pT = [nc.alloc_psum_tensor("pT%d" % i, [128, 1024], BF16).ap() for i in range(2)]
    bpT = [Buf("pT%d" % i) for i in range(2)]
    pM = [nc.alloc_psum_tensor("pM%d" % i, [128, 512], F32).ap() for i in range(6)]
    bpM = [Buf("pM%d" % i) for i in range(6)]
    rr = {"pT": 0, "pM": 0}

    def next_pT():
        i = rr["pT"] % 2
        rr["pT"] += 1
        return pT[i], bpT[i]

    def next_pM():
        i = rr["pM"] % 6
        rr["pM"] += 1
        return pM[i], bpM[i]

    R.dma("sp", ident, ident_in, wr=[b_ident], sb=b_ident)
    R.dma("sp", m2t, m2_in, wr=[b_m2], sb=b_m2)
    R.dma("sp", mkt, mk_in, wr=[b_mk], sb=b_mk)
    R.dma("sp", mmt, mm_in, wr=[b_mm], sb=b_mm)
    R.op("dve", lambda e: e.memset(ones32, 1.0), wr=[b_ones])
    wk = Carver(WREG, WORK_END)
    ct, b_ct = wk.get([128, 8], F32, "ct")
    cs, b_cs = wk.get([128, 8], F32, "cs")
    R.dma("sp", ct, cT, wr=[b_ct], sb=b_ct)
    R.op("act", lambda e: e.activation(out=cs, in_=ct, func=AF.Silu), rd=[b_ct], wr=[b_cs])
    for a in range(8):
        R.op("dve", lambda e, a=a: e.tensor_scalar(out=csb[:, a, :], in0=ones32, scalar1=cs[:, a:a + 1], scalar2=None,
                                                   op0=ALU.mult), rd=[b_ones, b_cs], wr=[b_csb])
    wv = Carver(0, WREG)
    wst = [wv.get([128, 2048], BF16, "wst%d" % i) for i in range(2)]
    mrow, b_mrow = wv.get([128, 2048], F32, "mrow")
    brow, b_brow = wv.get([128, 2048], F32, "brow")
    k = 0
    for l in range(4):
        for grp in range(3):
            cols = slice(grp * 2048, (grp + 1) * 2048)
            R.dma("sp", brow[0:1, :], b_ada[l:l + 1, cols], wr=[b_brow], sb=b_brow)
            ps = [next_pM() for _ in range(4)]
            for a in range(8):
                w, bw = wst[k % 2]
                k += 1
                R.dma("pool", w, w_ada[l, a * 128:(a + 1) * 128, cols], wr=[bw], sb=bw)
                for j in range(4):
                    R.op("pe", lambda e, j=j, a=a, w=w: e.matmul(ps[j][0], lhsT=csb[:, a, :], rhs=w[:, j * 512:(j + 1) * 512],
                                                                 start=(a == 0), stop=(a == 7)),
                         rd=[b_csb, bw], wr=[ps[j][1]], sig=(a == 7 or j == 3))
            for j in range(4):
                R.op("dve", lambda e, j=j: e.tensor_tensor(out=mrow[0:1, j * 512:(j + 1) * 512], in0=ps[j][0][0:1, :],
                                                           in1=brow[0:1, j * 512:(j + 1) * 512], op=ALU.add),
                     rd=[ps[j][1], b_brow], wr=[b_mrow])
            R.dma("sp", modrow[l:l + 1, cols], mrow[0:1, :], rd=[b_mrow], wr=[], sb=b_mrow)
    xt0, b_xt0 = wk.get([128, D], F32, "xt0")
    for q in range(NT):
        R.dma("sp", xt0, x_in[q * 128:(q + 1) * 128, :], wr=[b_xt0], sb=b_xt0)
        R.dma("sp", xs[q * 128:(q + 1) * 128, :], xt0, rd=[b_xt0], sb=b_xt0)
    R.barrier()

    def load_mod(l, which):
        base = which * 3 * D
        wkk = Carver(WREG, WORK_END)
        gt, b_gt = wkk.get([128, D], F32, "gtmp")
        R.dma("sp", modp.rearrange("p a b -> p (a b)"), modrow[l:l + 1, base:base + 3 * D].partition_broadcast(128),
              wr=[b_modp], sb=b_modp)
        R.dma("sp", gt, norm_g[l, which:which + 1, :].partition_broadcast(128), wr=[b_gt], sb=b_gt)
        R.op("dve", lambda e: e.scalar_tensor_tensor(out=modp[:, 1, :], in0=modp[:, 1, :], scalar=1.0, in1=gt,
                                                     op0=ALU.add, op1=ALU.mult), rd=[b_modp, b_gt], wr=[b_modp])
        R.barrier()

    class Work:
        pass

    def norm_tile(W, src_ap, slot, gA=None, gB=None):
        xt, bxt = W.xt[slot % len(W.xt)]
        R.dma("sp", xt, src_ap, wr=[bxt], sb=bxt)
        junk, bj = W.junk
        ss, bss = W.ss[slot % len(W.ss)]
        R.op("act", lambda e: e.activation(out=junk, in_=xt, func=AF.Square, accum_out=ss[:, 0:1]), rd=[bxt], wr=[bj, bss])
        R.op("dve", lambda e: e.tensor_scalar(out=ss[:, 1:2], in0=ss[:, 0:1], scalar1=1.0 / D, scalar2=EPS, op0=ALU.mult,
                                              op1=ALU.add), rd=[bss], wr=[bss])
        R.op("act", lambda e: e.activation(out=ss[:, 2:3], in_=ss[:, 1:2], func=AF.Sqrt), rd=[bss], wr=[bss])
        R.op("dve", lambda e: e.reciprocal(out=ss[:, 3:4], in_=ss[:, 2:3]), rd=[bss], wr=[bss])
        A = modp[:, 1, :] if gA is None else gA
        R.op("dve", lambda e: e.scalar_tensor_tensor(out=junk, in0=xt, scalar=ss[:, 3:4], in1=A, op0=ALU.mult, op1=ALU.mult),
             rd=[bxt, bss, b_modp], wr=[bj])
        hb, bhb = W.hb[slot % len(W.hb)]
        if gB is None:
            R.op("pool", lambda e: e.tensor_tensor(out=hb, in0=junk, in1=modp[:, 0, :], op=ALU.add), rd=[bj, b_modp], wr=[bhb])
        else:
            R.op("pool", lambda e: e.tensor_copy(out=hb, in_=junk), rd=[bj], wr=[bhb])
        return xt, bxt, hb, bhb

    def transpose_tile(src, bsrc, dst, bdst, nrows=128, ncols=128, dcol=None):
        p, bp = next_pT()
        for a in range(8):
            R.op("pe", lambda e, a=a: e.transpose(p[:, a * 128:a * 128 + nrows], src[0:nrows, a * 128:(a + 1) * 128],
                                                  ident[0:nrows, 0:nrows]),
                 rd=[bsrc, b_ident], wr=[bp], sig=(a == 7))
        pv = p.rearrange("p (a t) -> p a t", a=8)[:, :, 0:nrows]
        R.op("act", lambda e: e.activation(out=dst if dcol is None else dcol, in_=pv, func=AF.Copy), rd=[bp], wr=[bdst])

    def load_w(dst, bdst, src, eng="pool"):
        R.dma(eng, dst, src, wr=[bdst], sb=bdst)

    def proj_tm(lhsT, blhs, w, bw, ncol, K=8):
        outs = []
        for c0 in range(0, ncol, 512):
            cw = min(512, ncol - c0)
            p, bp = next_pM()
            for a in range(K):
                R.op("pe", lambda e, a=a, c0=c0, cw=cw, p=p: e.matmul(p[:, 0:cw], lhsT=lhsT[:, a, :], rhs=w[:, a, c0:c0 + cw],
                                                                    start=(a == 0), stop=(a == K - 1)),
                     rd=[blhs, bw], wr=[bp], sig=(a == K - 1))
            outs.append((p, bp, cw, c0))
        return outs

    def residual_out(W, outs, xt, bxt, q, final_dst=None):
        tmp, btmp = W.junk
        for (p, bp, cw, c0) in outs:
            R.op("dve", lambda e, p=p, c0=c0, cw=cw: e.tensor_tensor(out=tmp[:, c0:c0 + cw], in0=p[:, 0:cw],
                                                                      in1=modp[:, 2, c0:c0 + cw], op=ALU.mult),
                 rd=[bp, b_modp], wr=[btmp])
        R.op("pool", lambda e: e.tensor_tensor(out=xt, in0=xt, in1=tmp, op=ALU.add), rd=[btmp, bxt], wr=[bxt])
        R.dma("sp", xs[q * 128:(q + 1) * 128, :], xt, rd=[bxt], sb=bxt)

    def std_work(cv):
        W = Work()
        W.xt = [cv.get([128, D], F32, "xt%d" % i) for i in range(2)]
        W.junk = cv.get([128, D], F32, "junk")
        W.ss = [cv.get([128, 4], F32, "ss%d" % i) for i in range(2)]
        W.hb = [cv.get([128, D], BF16, "hb%d" % i) for i in range(2)]
        W.hT = [cv.get([128, 8, 128], BF16, "hT%d" % i) for i in range(2)]
        return W

    def phase_gmlp(l, j):
        load_mod(l, 0)
        wv = Carver(0, WREG)
        w_in, b_win = wv.get([128, 8, 2 * D], BF16, "a_w_in")
        w_out, b_wout = wv.get([128, 8, D], BF16, "a_w_out")
        w_s, b_ws = wv.get([128, 8, 128], BF16, "a_w_s")
        gv, b_gv = wv.get([128, D], F32, "a_g_v")
        bsT, b_bsT = wv.get([128, 8], F32, "a_b_sT")
        us = [wv.get([128, D], F32, "u%d" % i) for i in range(2)]
        vs_ = [wv.get([128, D], F32, "v%d" % i) for i in range(2)]
        vns = [wv.get([128, D], BF16, "vn%d" % i) for i in range(2)]
        gbs = [wv.get([128, D], BF16, "gated%d" % i) for i in range(2)]
        gTs = [wv.get([128, 8, 128], BF16, "gT%d" % i) for i in range(2)]
        vss = [wv.get([128, 8], F32, "vs%d" % i) for i in range(2)]
        for a in range(8):
            load_w(w_in[:, a, :], b_win, a_w_in[j, a * 128:(a + 1) * 128, :])
            load_w(w_out[:, a, :], b_wout, a_w_out[j, a * 128:(a + 1) * 128, :])
            load_w(w_s[:, a, :], b_ws, a_w_sT[j, a, :, :])
        R.dma("sp", gv, a_g_v[j:j + 1, :].partition_broadcast(128), wr=[b_gv], sb=b_gv)
        R.dma("sp", bsT, a_b_sT[j, :, :], wr=[b_bsT], sb=b_bsT)
        W = std_work(Carver(WREG, WORK_END))
        for q in range(NT):
            u, b_u = us[q % 2]
            v, b_v = vs_[q % 2]
            vn, b_vn = vns[q % 2]
            gb, b_gb = gbs[q % 2]
            gT, b_gT = gTs[q % 2]
            vs, b_vs = vss[q % 2]
            xt, bxt, hb, bhb = norm_tile(W, xs[q * 128:(q + 1) * 128, :], q)
            hT, bhT = W.hT[q % 2]
            transpose_tile(hb, bhb, hT, bhT)
            outs = proj_tm(hT, bhT, w_in, b_win, 2 * D)
            for i, (p, bp, cw, c0) in enumerate(outs):
                dst, bd = (u, b_u) if i < 2 else (v, b_v)
                cc = c0 % D
                R.op("act", lambda e, p=p, dst=dst, cc=cc: e.activation(out=dst[:, cc:cc + 512], in_=p, func=AF.Gelu),
                     rd=[bp], wr=[bd])
            R.op("act", lambda e: e.activation(out=vn, in_=v, func=AF.Square, accum_out=vs[:, 0:1]), rd=[b_v], wr=[b_vn, b_vs])
            R.op("dve", lambda e: e.tensor_scalar(out=vs[:, 1:2], in0=vs[:, 0:1], scalar1=1.0 / D, scalar2=EPS, op0=ALU.mult,
                                                  op1=ALU.add), rd=[b_vs], wr=[b_vs])
            R.op("act", lambda e: e.activation(out=vs[:, 2:3], in_=vs[:, 1:2], func=AF.Sqrt), rd=[b_vs], wr=[b_vs])
            R.op("dve", lambda e: e.reciprocal(out=vs[:, 3:4], in_=vs[:, 2:3]), rd=[b_vs], wr=[b_vs])
            R.op("dve", lambda e: e.scalar_tensor_tensor(out=vn, in0=v, scalar=vs[:, 3:4], in1=gv, op0=ALU.mult, op1=ALU.mult),
                 rd=[b_v, b_vs, b_gv], wr=[b_vn])
            for half in range(2):
                p, bp = next_pM()
                for gg in range(4):
                    g = half * 4 + gg
                    R.op("pe", lambda e, g=g, gg=gg, p=p: e.matmul(p[:, gg * 128:(gg + 1) * 128], lhsT=w_s[:, g, :],
                                                                  rhs=vn[:, g * 128:(g + 1) * 128], start=True, stop=True),
                         rd=[b_ws, b_vn], wr=[bp], sig=(gg == 3))
                for gg in range(4):
                    g = half * 4 + gg
                    R.op("dve", lambda e, g=g, gg=gg, p=p: e.scalar_tensor_tensor(
                        out=gb[:, g * 128:(g + 1) * 128], in0=p[:, gg * 128:(gg + 1) * 128], scalar=bsT[:, g:g + 1],
                        in1=u[:, g * 128:(g + 1) * 128], op0=ALU.add, op1=ALU.mult), rd=[bp, b_bsT, b_u], wr=[b_gb])
            transpose_tile(gb, b_gb, gT, b_gT)
            outs = proj_tm(gT, b_gT, w_out, b_wout, D)
            residual_out(W, outs, xt, bxt, q)
        R.barrier()

    def phase_ffn(l):
        TF = cfg.TF
        load_mod(l, 1)
        wv = Carver(0, WREG)
        w_up, b_wup = wv.get([128, 8, 2 * FF], BF16, "w_up")
        w_dn, b_wdn = wv.get([128, 22, D], BF16, "w_dn")
        for a in range(8):
            load_w(w_up[:, a, :], b_wup, f_w_up[l, a * 128:(a + 1) * 128, :])
        for jj in range(22):
            load_w(w_dn[:, jj, :], b_wdn, f_w_down[l, jj * 128:(jj + 1) * 128, :])
        cv = Carver(WREG, WORK_END)
        W = std_work(cv)
        hh, b_hh = cv.get([2, D], BF16, "hh")
        R.op("dve", lambda e: e.memset(hh, 0.0), wr=[b_hh])
        R.dma("sp", cpF[0:2, :], hh, rd=[b_hh], sb=b_hh)
        R.dma("sp", cpF[2 + 2 * NCORE:4 + 2 * NCORE, :], hh, rd=[b_hh], sb=b_hh)
        for q in range(NT):
            xt, bxt, hb, bhb = norm_tile(W, xs[q * 128:(q + 1) * 128, :], q)
            R.dma("sp", hbuf[1 + q * 128:1 + (q + 1) * 128, :], hb, rd=[bhb], sb=bhb)
            if q == 0:
                R.dma("sp", agF_in[0:1, :], hb[0:1, :], rd=[bhb], sb=bhb)
            if q == NT - 1:
                R.dma("sp", agF_in[1:2, :], hb[127:128, :], rd=[bhb], sb=bhb)
        R.barrier()
        b_ag = Buf("agF")
        R.allgather(agF_in, agF_out, sb=b_ag)
        R.barrier()
        R.dma("sp", cpF[2:2 + 2 * NCORE, :], agF_out, sb=Buf("cpF"))
        R.barrier()
        R.dma("sp", hh[0:2, :], cpF[bass.ds(pid * 2 + 1, 4), :][0:4:3, :], wr=[b_hh], sb=b_hh)
        R.op("dve", lambda e: e.tensor_scalar(out=hh, in0=hh, scalar1=m2t[:, 0:1], scalar2=None, op0=ALU.mult),
             rd=[b_hh, b_m2], wr=[b_hh])
        R.dma("sp", hbuf[0:1, :], hh[0:1, :], rd=[b_hh], sb=b_hh)
        R.dma("sp", hbuf[TPC + 1:TPC + 2, :], hh[1:2, :], rd=[b_hh], sb=b_hh)
        R.barrier()
        cv = Carver(WREG, WORK_END)
        xts = [cv.get([128, D], F32, "fxt%d" % i) for i in range(2)]
        tmpo = [cv.get([128, D], F32, "ftmp%d" % i) for i in range(1)]
        hms = [cv.get([128, D], BF16, "hm%d" % i) for i in range(2)]
        hX = [cv.get([128, 8, TF + 2], BF16, "hX%d" % i) for i in range(2)]
        actT, b_actT = cv.get([128, 22, TF], BF16, "actT")
        cas = [cv.get([128, TF], F32, "ca%d" % i) for i in range(2)]
        cgs = [cv.get([128, TF], F32, "cg%d" % i) for i in range(2)]
        sgs = [cv.get([128, TF], F32, "sg%d" % i) for i in range(2)]
        wc, b_wc = cv.get([128, 3, 44], F32, "wconv")
        bc, b_bc = cv.get([128, 44], F32, "bconv")
        R.dma("sp", wc, f_w_convT[l], wr=[b_wc], sb=b_wc)
        R.dma("sp", bc, f_b_convT[l], wr=[b_bc], sb=b_bc)
        tiles = []
        t0 = 0
        while t0 < TPC:
            n = min(TF, TPC - t0)
            tiles.append((t0, n))
            t0 += n
        cnt = {"hm": 0, "x": 0, "c": 0}

        def prep(i):
            t0, n = tiles[i]
            hx, bhx = hX[i % 2]
            tot = n + 2
            for s0 in range(0, tot, 128):
                nr = min(128, tot - s0)
                hm, bhm = hms[cnt["hm"] % 2]
                cnt["hm"] += 1
                R.dma("sp", hm[0:nr, :], hbuf[t0 + s0:t0 + s0 + nr, :], wr=[bhm], sb=bhm)
                transpose_tile(hm, bhm, hx, bhx, nrows=nr, dcol=hx[:, :, s0:s0 + nr])

        def up(i):
            t0, n = tiles[i]
            hx, bhx = hX[i % 2]
            for jj in range(22):
                k = cnt["c"] % 2
                cnt["c"] += 1
                ca, b_ca = cas[k]
                cg, b_cg = cgs[k]
                sg, b_sg = sgs[k]
                for part in range(2):
                    ch = jj + 22 * part
                    p, bp = next_pM()
                    for a in range(8):
                        R.op("pe", lambda e, a=a, ch=ch, p=p: e.matmul(p[:, 0:n + 2], lhsT=w_up[:, a, ch * 128:(ch + 1) * 128],
                                                                      rhs=hx[:, a, 0:n + 2], start=(a == 0), stop=(a == 7)),
                             rd=[b_wup, bhx], wr=[bp], sig=(a == 7))
                    dst, bd = (ca, b_ca) if part == 0 else (cg, b_cg)
                    R.op("act", lambda e, p=p, ch=ch, dst=dst: e.activation(out=dst[:, 0:n], in_=p[:, 1:n + 1], func=AF.Identity,
                                                                          bias=bc[:, ch:ch + 1], scale=wc[:, 1, ch:ch + 1]),
                         rd=[bp, b_wc, b_bc], wr=[bd])
                    R.op("dve", lambda e, p=p, ch=ch, dst=dst: e.scalar_tensor_tensor(
                        out=dst[:, 0:n], in0=p[:, 0:n], scalar=wc[:, 0, ch:ch + 1], in1=dst[:, 0:n], op0=ALU.mult, op1=ALU.add),
                        rd=[bp, b_wc, bd], wr=[bd])
                    R.op("dve", lambda e, p=p, ch=ch, dst=dst: e.scalar_tensor_tensor(
                        out=dst[:, 0:n], in0=p[:, 2:n + 2], scalar=wc[:, 2, ch:ch + 1], in1=dst[:, 0:n], op0=ALU.mult, op1=ALU.add),
                        rd=[bp, b_wc, bd], wr=[bd])
                R.op("act", lambda e, sg=sg, cg=cg: e.activation(out=sg[:, 0:n], in_=cg[:, 0:n], func=AF.Silu), rd=[b_cg], wr=[b_sg])
                R.op("pool", lambda e, jj=jj, ca=ca, sg=sg: e.tensor_tensor(out=actT[:, jj, 0:n], in0=ca[:, 0:n], in1=sg[:, 0:n],
                                                                            op=ALU.mult), rd=[b_ca, b_sg], wr=[b_actT])

        def down(i):
            t0, n = tiles[i]
            for s0 in range(0, n, 128):
                m = min(128, n - s0)
                xt, bxt = xts[cnt["x"] % 2]
                cnt["x"] += 1
                tmp, btmp = tmpo[0]
                R.dma("sp", xt[0:m, :], xs[t0 + s0:t0 + s0 + m, :], wr=[bxt], sb=bxt)
                for c0 in (0, 512):
                    p, bp = next_pM()
                    for jj in range(22):
                        R.op("pe", lambda e, jj=jj, p=p, c0=c0: e.matmul(p[0:m, :], lhsT=actT[:, jj, s0:s0 + m], rhs=w_dn[:, jj, c0:c0 + 512],
                                                                        start=(jj == 0), stop=(jj == 21)),
                             rd=[b_actT, b_wdn], wr=[bp], sig=(jj == 21))
                    R.op("dve", lambda e, p=p, c0=c0: e.tensor_tensor(out=tmp[0:m, c0:c0 + 512], in0=p[0:m, :], in1=modp[0:m, 2, c0:c0 + 512],
                                                                      op=ALU.mult), rd=[bp, b_modp], wr=[btmp])
                R.op("pool", lambda e, xt=xt, tmp=tmp: e.tensor_tensor(out=xt[0:m, :], in0=xt[0:m, :], in1=tmp[0:m, :], op=ALU.add),
                     rd=[btmp, bxt], wr=[bxt])
                R.dma("sp", xs[t0 + s0:t0 + s0 + m, :], xt[0:m, :], rd=[bxt], sb=bxt)

        prep(0)
        for i in range(len(tiles)):
            up(i)
            if i + 1 < len(tiles):
                prep(i + 1)
            down(i)
        R.barrier()

    def phase_attn(l):
        load_mod(l, 0)
        wv = Carver(0, WREG)
        w_qkv, b_wq = wv.get([128, 8, 1536], BF16, "w_qkv")
        w_o, b_wo = wv.get([128, 8, D], BF16, "w_o")
        bias, b_bias = wv.get([128, 16, 384], F32, "bias")
        dist, b_dist = wv.get([128, 384], F32, "dist")
        amk, b_amk = wv.get([128, 384], F32, "amk")
        snk, b_snk = wv.get([128, 16], F32, "snk")
        kvt = [wv.get([128, 512], BF16, "kvt%d" % i) for i in range(2)]
        kh, b_kh = wv.get([128, 2, 512], BF16, "kh")
        zt, b_zt = wv.get([128, 512], BF16, "zt")
        kb = [wv.get([128, 3, 512], BF16, "kb%d" % i) for i in range(2)]
        kT, b_kT = wv.get([128, 2, 384], BF16, "kT")
        qb, b_qb = wv.get([128, D], BF16, "qb")
        qT, b_qT = wv.get([128, 8, 128], BF16, "qT")
        sc = [wv.get([128, 384], F32, "sc%d" % i) for i in range(2)]
        eb = [wv.get([128, 384], BF16, "eb%d" % i) for i in range(2)]
        ebT = [wv.get([128, 3, 128], BF16, "ebT%d" % i) for i in range(2)]
        st = [wv.get([128, 8], F32, "st%d" % i) for i in range(2)]
        ob, b_ob = wv.get([128, D], BF16, "ob")
        oT, b_oT = wv.get([128, 8, 128], BF16, "oT")
        for a in range(8):
            load_w(w_qkv[:, a, :], b_wq, b_w_qkv[a * 128:(a + 1) * 128, :])
            load_w(w_o[:, a, :], b_wo, b_w_o[a * 128:(a + 1) * 128, :])
        R.dma("sp", dist, adist, wr=[b_dist], sb=b_dist)
        R.dma("sp", amk, amask, wr=[b_amk], sb=b_amk)
        R.dma("sp", snk, b_sinks[0:1, :].partition_broadcast(128), wr=[b_snk], sb=b_snk)
        for h in range(16):
            slope = float(2.0 ** (-8.0 * (h + 1) / 16))
            R.op("dve", lambda e, h=h, slope=slope: e.scalar_tensor_tensor(out=bias[:, h, :], in0=dist, scalar=-slope, in1=amk,
                                                                           op0=ALU.mult, op1=ALU.add),
                 rd=[b_dist, b_amk], wr=[b_bias])
        W = std_work(Carver(WREG, WORK_END))
        R.op("dve", lambda e: e.memset(zt, 0.0), wr=[b_zt])
        for blk in (0, NCORE + 1):
            for hf in range(2):
                R.dma("sp", cpK[blk * 256 + hf * 128:blk * 256 + (hf + 1) * 128, :], zt, rd=[b_zt], sb=b_zt)
        for q in range(NT):
            xt, bxt, hb, bhb = norm_tile(W, xs[q * 128:(q + 1) * 128, :], q)
            hT, bhT = W.hT[q % 2]
            transpose_tile(hb, bhb, hT, bhT)
            p, bp = next_pM()
            for a in range(8):
                R.op("pe", lambda e, a=a, p=p: e.matmul(p, lhsT=hT[:, a, :], rhs=w_qkv[:, a, 1024:1536], start=(a == 0), stop=(a == 7)),
                     rd=[bhT, b_wq], wr=[bp], sig=(a == 7))
            kv, bkv = kvt[q % 2]
            R.op("act", lambda e, p=p, kv=kv: e.activation(out=kv, in_=p, func=AF.Copy), rd=[bp], wr=[bkv])
            R.dma("sp", kvbuf[128 + q * 128:128 + (q + 1) * 128, :], kv, rd=[bkv], sb=bkv)
            if q == 0:
                R.dma("sp", agK_in[0:128, :], kv, rd=[bkv], sb=bkv)
            if q == NT - 1:
                R.dma("sp", agK_in[128:256, :], kv, rd=[bkv], sb=bkv)
        R.barrier()
        b_ag = Buf("agK")
        R.allgather(agK_in, agK_out, sb=b_ag)
        R.barrier()
        R.dma("sp", cpK[256:256 + 256 * NCORE, :], agK_out, sb=Buf("cpK"))
        R.barrier()
        R.dma("sp", kh, cpK.rearrange("(b p) c -> p b c", p=128)[:, bass.ds(pid * 2 + 1, 4), :][:, 0:4:3, :], wr=[b_kh], sb=b_kh)
        for sd in range(2):
            R.op("dve", lambda e, sd=sd: e.tensor_scalar(out=kh[:, sd, :], in0=kh[:, sd, :], scalar1=mmt[:, sd:sd + 1], scalar2=None,
                                                         op0=ALU.mult), rd=[b_kh, b_mm], wr=[b_kh])
        R.dma("sp", kvbuf[0:128, :], kh[:, 0, :], rd=[b_kh], sb=b_kh)
        R.dma("sp", kvbuf[TPC + 128:TPC + 256, :], kh[:, 1, :], rd=[b_kh], sb=b_kh)
        R.barrier()
        for q in range(NT):
            xt, bxt, hb, bhb = norm_tile(W, xs[q * 128:(q + 1) * 128, :], q)
            hT, bhT = W.hT[q % 2]
            transpose_tile(hb, bhb, hT, bhT)
            kbt, bkb = kb[q % 2]
            R.dma("sp", kbt, kvbuf[q * 128:q * 128 + 384, :].rearrange("(b p) c -> p b c", p=128), wr=[bkb], sb=bkb)
            for g in range(2):
                p, bp = next_pM()
                for a in range(8):
                    R.op("pe", lambda e, a=a, p=p, g=g: e.matmul(p, lhsT=hT[:, a, :], rhs=w_qkv[:, a, g * 512:(g + 1) * 512],
                                                                start=(a == 0), stop=(a == 7)),
                         rd=[bhT, b_wq], wr=[bp], sig=(a == 7))
                R.op("act", lambda e, p=p, g=g: e.activation(
                    out=qb[:, g * 512:(g + 1) * 512].rearrange("p (i k d) -> p k i d", i=4, k=2, d=64),
                    in_=p.rearrange("p (k i d) -> p k i d", k=2, i=4, d=64), func=AF.Copy), rd=[bp], wr=[b_qb])
            transpose_tile(qb, b_qb, qT, b_qT)
            p, bp = next_pT()
            for c2 in range(2):
                for b in range(3):
                    R.op("pe", lambda e, c2=c2, b=b, p=p: e.transpose(p[:, (c2 * 3 + b) * 128:(c2 * 3 + b + 1) * 128],
                                                                     kbt[:, b, c2 * 128:(c2 + 1) * 128], ident),
                         rd=[bkb, b_ident], wr=[bp], sig=(c2 == 1 and b == 2))
            R.op("act", lambda e, p=p: e.activation(out=kT.rearrange("p c k -> p (c k)"), in_=p[:, 0:768], func=AF.Copy),
                 rd=[bp], wr=[b_kT])
            po = [next_pM(), next_pM()]
            for h in range(16):
                g, k, i = h // 8, (h // 4) % 2, h % 4
                kvh = h // 4
                s_, bs_ = sc[h % 2]
                e_, be_ = eb[h % 2]
                eT, beT = ebT[h % 2]
                t_, bt_ = st[h % 2]
                p, bp = next_pM()
                R.op("pe", lambda e, p=p, g=g, k=k, i=i: e.matmul(p[:, 0:384], lhsT=qT[k * 64:(k + 1) * 64, g * 4 + i, :],
                                                                 rhs=kT[k * 64:(k + 1) * 64, g, :], start=True, stop=True),
                     rd=[b_qT, b_kT], wr=[bp])
                R.op("dve", lambda e, p=p, h=h, s_=s_: e.scalar_tensor_tensor(out=s_, in0=p[:, 0:384], scalar=0.125, in1=bias[:, h, :],
                                                                              op0=ALU.mult, op1=ALU.add),
                     rd=[bp, b_bias], wr=[bs_])
                if q == 0:
                    R.op("dve", lambda e, s_=s_: e.tensor_scalar(out=s_[:, 0:128], in0=s_[:, 0:128], scalar1=mkt[:, 0:1], scalar2=None,
                                                                 op0=ALU.add), rd=[bs_, b_mk], wr=[bs_])
                if q == NT - 1:
                    R.op("dve", lambda e, s_=s_: e.tensor_scalar(out=s_[:, 256:384], in0=s_[:, 256:384], scalar1=mkt[:, 1:2],
                                                                 scalar2=None, op0=ALU.add), rd=[bs_, b_mk], wr=[bs_])
                R.op("dve", lambda e, s_=s_, t_=t_: e.reduce_max(out=t_[:, 0:1], in_=s_, axis=AX.X), rd=[bs_], wr=[bt_])
                R.op("dve", lambda e, t_=t_, h=h: e.tensor_tensor(out=t_[:, 1:2], in0=t_[:, 0:1], in1=snk[:, h:h + 1], op=ALU.max),
                     rd=[bt_, b_snk], wr=[bt_])
                R.op("dve", lambda e, t_=t_: e.tensor_scalar(out=t_[:, 2:3], in0=t_[:, 1:2], scalar1=-1.0, scalar2=None, op0=ALU.mult),
                     rd=[bt_], wr=[bt_])
                R.op("act", lambda e, s_=s_, e_=e_, t_=t_: e.activation(out=e_, in_=s_, func=AF.Exp, bias=t_[:, 2:3], scale=1.0,
                                                                        accum_out=t_[:, 3:4]), rd=[bs_, bt_], wr=[be_, bt_])
                R.op("act", lambda e, t_=t_, h=h: e.activation(out=t_[:, 4:5], in_=snk[:, h:h + 1], func=AF.Exp, bias=t_[:, 2:3],
                                                               scale=1.0), rd=[b_snk, bt_], wr=[bt_])
                R.op("dve", lambda e, t_=t_: e.tensor_tensor(out=t_[:, 5:6], in0=t_[:, 3:4], in1=t_[:, 4:5], op=ALU.add),
                     rd=[bt_], wr=[bt_])
                R.op("dve", lambda e, t_=t_: e.reciprocal(out=t_[:, 6:7], in_=t_[:, 5:6]), rd=[bt_], wr=[bt_])
                pt, bpt = next_pT()
                for b in range(3):
                    R.op("pe", lambda e, b=b, pt=pt, e_=e_: e.transpose(pt[:, b * 128:(b + 1) * 128], e_[:, b * 128:(b + 1) * 128], ident),
                         rd=[be_, b_ident], wr=[bpt], sig=(b == 2))
                R.op("act", lambda e, pt=pt, eT=eT: e.activation(out=eT.rearrange("p b k -> p (b k)"), in_=pt[:, 0:384], func=AF.Copy),
                     rd=[bpt], wr=[beT])
                pp, bpp = po[h // 8]
                sl = slice((h % 8) * 64, (h % 8 + 1) * 64)
                for b in range(3):
                    R.op("pe", lambda e, b=b, pp=pp, sl=sl, eT=eT, kvh=kvh: e.matmul(pp[:, sl], lhsT=eT[:, b, :],
                                                                                     rhs=kbt[:, b, 256 + kvh * 64:256 + (kvh + 1) * 64],
                                                                                     start=(b == 0), stop=(b == 2)),
                         rd=[beT, bkb], wr=[bpp], sig=(b == 2))
                R.op("dve", lambda e, pp=pp, sl=sl, h=h, t_=t_: e.tensor_scalar(out=ob[:, h * 64:(h + 1) * 64], in0=pp[:, sl],
                                                                                scalar1=t_[:, 6:7], scalar2=None, op0=ALU.mult),
                     rd=[bpp, bt_], wr=[b_ob])
            transpose_tile(ob, b_ob, oT, b_oT)
            outs = proj_tm(oT, b_oT, w_o, b_wo, D)
            residual_out(W, outs, xt, bxt, q)
        R.barrier()

    def phase_fft(l):
        load_mod(l, 0)
        W = std_work(Carver(WREG, WORK_END))
        for q in range(NT):
            xt, bxt, hb, bhb = norm_tile(W, xs[q * 128:(q + 1) * 128, :], q)
            hT, bhT = W.hT[q % 2]
            transpose_tile(hb, bhb, hT, bhT)
            R.dma("sp", agH_in[q * 128:(q + 1) * 128, :], hT.rearrange("p a t -> p (a t)"), rd=[bhT], sb=bhT)
        R.barrier()
        b_ag = Buf("agH")
        R.allgather(agH_in, agH_out, sb=b_ag)
        R.barrier()
        wv = Carver(0, WREG)
        wgT, b_wgT = wv.get([128, D], BF16, "wgT")
        csm, b_csm = wv.get([128, 256], BF16, "csm")
        wz, b_wz = wv.get([128, 8, 256], BF16, "wz")
        hTt = [wv.get([128, 8, 128], BF16, "hTt%d" % i) for i in range(3)]
        ztl = [wv.get([128, 256], BF16, "ztl%d" % i) for i in range(3)]
        load_w(wgT, b_wgT, c_w_inT)
        R.dma("sp", csm, cs128, wr=[b_csm], sb=b_csm)
        for a in range(8):
            p, bp = next_pM()
            R.op("pe", lambda e, a=a, p=p: e.matmul(p[:, 0:256], lhsT=wgT[:, a * 128:(a + 1) * 128], rhs=csm, start=True, stop=True),
                 rd=[b_wgT, b_csm], wr=[bp])
            R.op("act", lambda e, a=a, p=p: e.activation(out=wz[:, a, :], in_=p[:, 0:256], func=AF.Copy), rd=[bp], wr=[b_wz])
        for gq in range(NTOK // 128):
            ht, bht = hTt[gq % 3]
            R.dma("sp", ht.rearrange("p a t -> p (a t)"), agH_out[gq * 128:(gq + 1) * 128, :], wr=[bht], sb=bht)
            p, bp = next_pM()
            for a in range(8):
                R.op("pe", lambda e, a=a, p=p, ht=ht: e.matmul(p[:, 0:256], lhsT=ht[:, a, :], rhs=wz[:, a, :], start=(a == 0), stop=(a == 7)),
                     rd=[bht, b_wz], wr=[bp], sig=(a == 7))
            z_, bz_ = ztl[gq % 3]
            R.op("act", lambda e, p=p, z_=z_: e.activation(out=z_, in_=p[:, 0:256], func=AF.Copy), rd=[bp], wr=[bz_])
            R.dma("sp", zbuf[gq * 128:(gq + 1) * 128, :], z_, rd=[bz_], sb=bz_)
        R.barrier()
        for (st0, S) in cfg.seqs:
            N1 = S // 128
            t2c_d, t2s_d, t1a_d, t1b_d = tabs[S]
            fv = Carver(0, WORK_END)
            zz, b_zz = fv.get([128, 128, 256], BF16, "zz")
            U, b_U = fv.get([128, N1, 2, 128], BF16, "U")
            t1a, b_t1a = fv.get([128, 2 * N1], BF16, "t1a")
            t1b, b_t1b = fv.get([128, 2 * N1], BF16, "t1b")
            KC = min(16, N1)
            tc_ = [fv.get([128, KC * 128], BF16, "tc%d" % i) for i in range(2)]
            ts_ = [fv.get([128, KC * 128], BF16, "ts%d" % i) for i in range(2)]
            ys = [fv.get([128, KC, 128], BF16, "ys%d" % i) for i in range(2)]
            R.dma("sp", zz[0:N1].rearrange("p s c -> p (s c)"),
                  zbuf[st0:st0 + S, :].rearrange("(s1 s2) c -> s1 (s2 c)", s2=128), wr=[b_zz], sb=b_zz)
            R.dma("sp", t1a[0:N1, :], t1a_d, wr=[b_t1a], sb=b_t1a)
            R.dma("sp", t1b[0:N1, :], t1b_d, wr=[b_t1b], sb=b_t1b)
            cpb = 512 // (2 * N1)
            for c0 in range(0, 128, cpb):
                p, bp = next_pM()
                for ci in range(cpb):
                    c = c0 + ci
                    o_ = p[:, ci * 2 * N1:(ci + 1) * 2 * N1]
                    R.op("pe", lambda e, o_=o_, c=c: e.matmul(o_, lhsT=zz[0:N1, :, c], rhs=t1a[0:N1, :], start=True, stop=False),
                         rd=[b_zz, b_t1a], wr=[bp], sig=False)
                    R.op("pe", lambda e, o_=o_, c=c: e.matmul(o_, lhsT=zz[0:N1, :, 128 + c], rhs=t1b[0:N1, :], start=False, stop=True),
                         rd=[b_zz, b_t1b], wr=[bp], sig=(ci == cpb - 1))
                R.op("act", lambda e, p=p, c0=c0: e.activation(
                    out=U[:, :, :, c0:c0 + cpb].rearrange("p k r c -> p c r k"),
                    in_=p[:, 0:cpb * 2 * N1].rearrange("p (c r k) -> p c r k", c=cpb, r=2), func=AF.Copy), rd=[bp], wr=[b_U])
            scale = float(1.0 / np.sqrt(S * 128.0))
            ydst = agY_in[st0:st0 + S, :].rearrange("(k2 k1) c -> k2 k1 c", k1=N1)
            for kc in range(0, N1, KC):
                tcc, btc = tc_[(kc // KC) % 2]
                tss, bts = ts_[(kc // KC) % 2]
                yy, byy = ys[(kc // KC) % 2]
                R.dma("sp", tcc, t2c_d[:, kc * 128:(kc + KC) * 128], wr=[btc], sb=btc)
                R.dma("sp", tss, t2s_d[:, kc * 128:(kc + KC) * 128], wr=[bts], sb=bts)
                for k4 in range(0, KC, 4):
                    p, bp = next_pM()
                    nk = min(4, KC - k4)
                    for kk in range(nk):
                        k1 = kc + k4 + kk
                        o_ = p[:, kk * 128:(kk + 1) * 128]
                        R.op("pe", lambda e, o_=o_, k1=k1, kk=kk, k4=k4, tcc=tcc: e.matmul(
                            o_, lhsT=tcc[:, (k4 + kk) * 128:(k4 + kk + 1) * 128], rhs=U[:, k1, 0, :], start=True, stop=False),
                            rd=[btc, b_U], wr=[bp], sig=False)
                        R.op("pe", lambda e, o_=o_, k1=k1, kk=kk, k4=k4, tss=tss: e.matmul(
                            o_, lhsT=tss[:, (k4 + kk) * 128:(k4 + kk + 1) * 128], rhs=U[:, k1, 1, :], start=False, stop=True),
                            rd=[bts, b_U], wr=[bp], sig=(kk == nk - 1))
                    R.op("act", lambda e, p=p, k4=k4, nk=nk, yy=yy: e.activation(
                        out=yy[:, k4:k4 + nk, :].rearrange("p k c -> p (k c)"), in_=p[:, 0:nk * 128], func=AF.Copy, scale=scale),
                        rd=[bp], wr=[byy])
                R.dma("sp", ydst[:, kc:kc + KC, :], yy, rd=[byy], sb=byy)
            R.barrier()
        b_agy = Buf("agY")
        R.allgather(agY_in, agY_out, sb=b_agy)
        R.barrier()
        nrow = NCORE * NTOK
        step = nrow // 8
        b_cpy = Buf("cpY")
        for i in range(8):
            R.dma("sp", cpY[i * step:(i + 1) * step, :], agY_out[i * step:(i + 1) * step, :], sb=b_cpy)
        R.barrier()
        wv = Carver(0, WREG)
        w_out, b_wout = wv.get([128, 8, D], BF16, "c_w_out")
        ft = [wv.get([128, 8, 128], BF16, "ft%d" % i) for i in range(2)]
        fT, b_fT = wv.get([128, 8, 128], BF16, "fT")
        for a in range(8):
            load_w(w_out[:, a, :], b_wout, c_w_out[a * 128:(a + 1) * 128, :])
        cpYv = cpY.rearrange("(g t) c -> g t c", g=NCORE)
        b_fown = Buf("fown")
        for g in range(NCORE):
            R.dma("act", fown[g], cpYv[g, bass.ds(pid_act * TPC, TPC), :], sb=b_fown)
        R.barrier()
        for q in range(NT):
            f_, bf_ = ft[q % 2]
            R.dma("sp", f_, fown[:, q * 128:(q + 1) * 128, :].rearrange("g t c -> t g c"), wr=[bf_], sb=bf_)
            xt, bxt = W.xt[q % 2]
            R.dma("sp", xt, xs[q * 128:(q + 1) * 128, :], wr=[bxt], sb=bxt)
            transpose_tile(f_.rearrange("p g c -> p (g c)"), bf_, fT, b_fT)
            outs = proj_tm(fT, b_fT, w_out, b_wout, D)
            residual_out(W, outs, xt, bxt, q)
        R.barrier()

    def final_phase():
        wkk = Carver(WREG, WORK_END)
        W = std_work(wkk)
        gf, b_gf = Carver(0, WREG).get([128, D], F32, "gf")
        R.dma("sp", gf, g_final[0:1, :].partition_broadcast(128), wr=[b_gf], sb=b_gf)
        for q in range(NT):
            xt, bxt = W.xt[q % 2]
            R.dma("sp", xt, xs[q * 128:(q + 1) * 128, :], wr=[bxt], sb=bxt)
            junk, bj = W.junk
            ss, bss = W.ss[q % 2]
            R.op("act", lambda e: e.activation(out=junk, in_=xt, func=AF.Square, accum_out=ss[:, 0:1]), rd=[bxt], wr=[bj, bss])
            R.op("dve", lambda e: e.tensor_scalar(out=ss[:, 1:2], in0=ss[:, 0:1], scalar1=1.0 / D, scalar2=EPS, op0=ALU.mult,
                                                  op1=ALU.add), rd=[bss], wr=[bss])
            R.op("act", lambda e: e.activation(out=ss[:, 2:3], in_=ss[:, 1:2], func=AF.Sqrt), rd=[bss], wr=[bss])
            R.op("dve", lambda e: e.reciprocal(out=ss[:, 3:4], in_=ss[:, 2:3]), rd=[bss], wr=[bss])
            R.op("dve", lambda e: e.scalar_tensor_tensor(out=xt, in0=xt, scalar=ss[:, 3:4], in1=gf, op0=ALU.mult, op1=ALU.mult),
                 rd=[bxt, bss, b_gf], wr=[bxt])
            R.dma("sp", y_out[q * 128:(q + 1) * 128, :], xt, rd=[bxt], sb=bxt)
        R.barrier()

    stages = debug or "all"
    plan = {"g0": ["g0"], "g0f0": ["g0", "f0"], "l1m": ["g0", "f0", "a1"], "l1": ["g0", "f0", "a1", "f1"],
            "l2m": ["g0", "f0", "a1", "f1", "c2"], "l2": ["g0", "f0", "a1", "f1", "c2", "f2"],
            "all": ["g0", "f0", "a1", "f1", "c2", "f2", "g3", "f3"]}[stages]
    for ph in plan:
        if ph[0] == "g":
            phase_gmlp(int(ph[1]), int(ph[1]) // 3)
        elif ph[0] == "f":
            phase_ffn(int(ph[1]))
        elif ph[0] == "a":
            phase_attn(int(ph[1]))
        elif ph[0] == "c":
            phase_fft(int(ph[1]))
    if stages != "all":
        W = std_work(Carver(WREG, WORK_END))
        for q in range(NT):
            xt, bxt = W.xt[q % 2]
            R.dma("sp", xt, xs[q * 128:(q + 1) * 128, :], wr=[bxt], sb=bxt)
            R.dma("sp", dbg_out[q * 128:(q + 1) * 128, :], xt, rd=[bxt], sb=bxt)
        R.barrier()
    final_phase()
    return nc


def core_inputs(cfg, r, inp, x_all, c_all):
    TPC = cfg.TPC
    seq = 0 if r < 2 else (1 if r < 4 else 2)
    first = r in (0, 2, 4)
    last = r in (1, 3, 7)
    c = c_all[seq]
    m = {}
    m["x"] = np.ascontiguousarray(x_all[r * TPC:(r + 1) * TPC])
    m["cT"] = np.ascontiguousarray(c.reshape(8, 128).T)
    for kname in ("w_ada", "b_ada", "norm_g", "a_w_in", "a_g_v", "a_w_out", "c_w_out", "f_w_up", "f_w_down"):
        m[kname] = inp[kname]
    m["a_w_sT"] = np.ascontiguousarray(np.transpose(inp["a_w_s"], (0, 1, 3, 2)))
    m["a_b_sT"] = np.ascontiguousarray(np.transpose(inp["a_b_s"], (0, 2, 1)))
    m["b_w_qkv"] = inp["b_w_qkv"][0]
    m["b_sinks"] = inp["b_sinks"]
    m["b_w_o"] = inp["b_w_o"][0]
    m["c_w_inT"] = np.ascontiguousarray(inp["c_w_in"][0][:, r * 128:(r + 1) * 128].T)
    m["c_w_out"] = inp["c_w_out"][0]
    m["f_w_convT"] = np.ascontiguousarray(np.transpose(inp["f_w_conv"].reshape(4, 3, 44, 128), (0, 3, 1, 2)))
    m["f_b_convT"] = np.ascontiguousarray(np.transpose(inp["f_b_conv"].reshape(4, 44, 128), (0, 2, 1)))
    m["g_final"] = inp["g_final"].reshape(1, D)
    m["ident"] = bf(np.eye(128))
    m["m2"] = np.array([[0.0 if first else 1.0], [0.0 if last else 1.0]], np.float32)
    mk = np.zeros((128, 2), np.float32)
    mk[:, 0] = NEG if first else 0.0
    mk[:, 1] = NEG if last else 0.0
    m["mk"] = mk
    mm = np.ones((128, 2), np.float32)
    mm[:, 0] = 0.0 if first else 1.0
    mm[:, 1] = 0.0 if last else 1.0
    m["mm"] = mm
    qp = np.arange(128)[:, None]
    kp = np.arange(384)[None, :] - 128
    dist = np.abs(qp - kp).astype(np.float32)
    m["adist"] = dist
    m["amask"] = np.where(dist <= 128, 0.0, NEG).astype(np.float32)
    cc = np.arange(128, dtype=np.float64)
    ph = 2 * np.pi * ((cc[:, None] * cc[None, :]) % 128) / 128
    m["cs128"] = bf(np.concatenate([np.cos(ph), np.sin(ph)], axis=1))
    done = set()
    for (st, S) in cfg.seqs:
        if S in done:
            continue
        done.add(S)
        t2c, t2s, t1a, t1b = dft_tables(S)
        m["t2c_%d" % S], m["t2s_%d" % S], m["t1a_%d" % S], m["t1b_%d" % S] = t2c, t2s, t1a, t1b
    return m


def run(cfg, inp, x_all, c_all, debug=None):
    nc = build(cfg, debug)
    in_maps = [core_inputs(cfg, r, inp, x_all, c_all) for r in range(NCORE)]
    res = run_bass_kernel_spmd(nc, in_maps, core_ids=list(range(NCORE)))
    y = np.concatenate([res.results[r]["y"] for r in range(NCORE)], axis=0)
    dbg = None
    if debug:
        dbg = np.concatenate([res.results[r]["dbg"] for r in range(NCORE)], axis=0)
    return y, dbg


def kernel(**inputs):
    inp = {k: np.asarray(v) for k, v in inputs.items()}
    xp, xsm = inp["x_prompt"], inp["x_sample"]
    B, S, _ = xp.shape
    tpc = S // 2
    cfg = Cfg(tpc)
    x_all = np.concatenate([xp.reshape(B * S, D), xsm.reshape(-1, D)], axis=0)
    c_all = [inp["c_prompt"][0], inp["c_prompt"][1], inp["c_sample"][0]]
    y, _ = run(cfg, inp, x_all, c_all)
    y_prompt = y[:B * S].reshape(B, S, D).astype(np.float32)
    y_sample = y[B * S:].reshape(xsm.shape).astype(np.float32)
    return (y_prompt, y_sample)
```

```python
import numpy as np
import ml_dtypes
import concourse.bass as bass
import concourse.mybir as mybir
from concourse.bass_utils import run_bass_kernel_spmd

F32 = mybir.dt.float32
BF16 = mybir.dt.bfloat16
I32 = mybir.dt.int32
AF = mybir.ActivationFunctionType
ALU = mybir.AluOpType
AX = mybir.AxisListType

D = 1024
FF = 2816
NCORE = 8
EPS = 1e-6
NEG = -1e30


class Tok:
    __slots__ = ("sem", "val", "eng", "buf")

    def __init__(self, sem, val, eng, buf=None):
        self.sem, self.val, self.eng, self.buf = sem, val, eng, buf


class Buf:
    def __init__(self, name):
        self.name = name
        self.lw = None
        self.rd = []
        self.dsem = None
        self.dcnt = 0


class Rec:
    def __init__(self, nc):
        self.nc = nc
        self.E = {"pe": nc.tensor, "dve": nc.vector, "act": nc.scalar, "pool": nc.gpsimd, "sp": nc.sync}
        self.sem = {e: nc.alloc_semaphore("s_" + e) for e in self.E}
        self.cnt = {e: 0 for e in self.E}
        self.seen = {e: {} for e in self.E}
        self.pending = {e: [] for e in self.E}
        self.dbufs = []
        self.nsem = 0
        self.pools = {"d": [], "g": [], "c": []}

    def _getsem(self, sb, prefix):
        sb.kind = prefix
        if self.pools[prefix]:
            sb.dsem, sb.dcnt = self.pools[prefix].pop()
        else:
            sb.dsem = self.nc.alloc_semaphore("%s_%d" % (prefix, self.nsem))
            self.nsem += 1
            sb.dcnt = 0
        self.dbufs.append(sb)

    def _wait(self, eng, t):
        if t is None:
            return
        if t.buf is not None:
            if t.buf.dsem is not t.sem:
                return
            val = t.buf.dcnt
        else:
            if t.eng == eng and eng == "pe":
                return
            val = t.val
            assert val is not None, "dependency on un-signalled instruction"
        sid = id(t.sem)
        if self.seen[eng].get(sid, 0) >= val:
            return
        self.E[eng].wait_ge(t.sem, val)
        self.seen[eng][sid] = val

    def _deps(self, eng, rd, wr):
        for b in rd:
            self._wait(eng, b.lw)
        for b in wr:
            self._wait(eng, b.lw)
            for t in b.rd:
                self._wait(eng, t)

    def _upd(self, tok, rd, wr):
        for b in wr:
            b.lw = tok
            b.rd = []
        for b in rd:
            b.rd = [t for t in b.rd if t.sem is not tok.sem] + [tok]

    def op(self, eng, fn, rd=(), wr=(), sig=True):
        self._deps(eng, rd, wr)
        ins = fn(self.E[eng])
        tok = Tok(self.sem[eng], None, eng)
        self.pending[eng].append(tok)
        if sig:
            self.cnt[eng] += 1
            ins.then_inc(self.sem[eng], 1)
            for t in self.pending[eng]:
                t.val = self.cnt[eng]
            self.pending[eng] = []
        self._upd(tok, rd, wr)
        return tok

    def dma(self, eng, out, in_, rd=(), wr=(), sb=None):
        self._deps(eng, rd, wr)
        kind = "g" if eng == "pool" else "d"
        if sb.dsem is None:
            self._getsem(sb, kind)
        assert sb.kind == kind, ("semaphore kind mix", sb.name)
        ins = self.E[eng].dma_start(out=out, in_=in_)
        sb.dcnt += 16
        ins.then_inc(sb.dsem, 16)
        tok = Tok(sb.dsem, sb.dcnt, None, sb)
        self._upd(tok, rd, wr)
        return tok

    def allgather(self, in_ap, out_ap, rd=(), wr=(), sb=None):
        eng = "pool"
        self._deps(eng, rd, wr)
        if sb.dsem is None:
            self._getsem(sb, "c")
        assert sb.kind == "c"
        ins = self.nc.gpsimd.collective_compute(
            "AllGather", ALU.bypass, replica_groups=[list(range(NCORE))], ins=[in_ap.opt()], outs=[out_ap.opt()]
        )
        sb.dcnt += 1
        ins.then_inc(sb.dsem)
        tok = Tok(sb.dsem, sb.dcnt, None, sb)
        self._upd(tok, rd, wr)
        return tok

    def barrier(self):
        for e in self.E:
            assert not self.pending[e], "barrier with un-signalled instructions on " + e
        for e in self.E:
            for o in self.E:
                if o == e:
                    continue
                v = self.cnt[o]
                if v and self.seen[e].get(id(self.sem[o]), 0) < v:
                    self.E[e].wait_ge(self.sem[o], v)
                    self.seen[e][id(self.sem[o])] = v
            for b in self.dbufs:
                if b.dcnt and self.seen[e].get(id(b.dsem), 0) < b.dcnt:
                    self.E[e].wait_ge(b.dsem, b.dcnt)
                    self.seen[e][id(b.dsem)] = b.dcnt
        for b in self.dbufs:
            self.pools[b.kind].append((b.dsem, b.dcnt))
            b.dsem = None
            b.dcnt = 0
        self.dbufs = []


class Cfg:
    def __init__(self, tpc):
        self.TPC = tpc
        self.NT = tpc // 128
        self.NTOK = tpc * NCORE
        self.TF = min(256, tpc)
        self.seqs = [(0, 2 * tpc), (2 * tpc, 2 * tpc), (4 * tpc, 4 * tpc)]


def bf(a):
    return np.ascontiguousarray(np.asarray(a, np.float32)).astype(ml_dtypes.bfloat16)


def dft_tables(S):
    N1 = S // 128
    s2 = np.arange(128, dtype=np.float64)[:, None, None]
    k1 = np.arange(N1, dtype=np.float64)[None, :, None]
    k2 = np.arange(128, dtype=np.float64)[None, None, :]
    ph = 2 * np.pi * ((s2 * (k1 + N1 * k2)) % S) / S
    c2, s2t = np.cos(ph), -np.sin(ph)
    a = np.arange(N1, dtype=np.float64)
    ph1 = 2 * np.pi * ((a[:, None] * a[None, :]) % N1) / N1
    t1a = np.concatenate([np.cos(ph1), np.sin(ph1)], axis=1)
    t1b = np.concatenate([-np.sin(ph1), np.cos(ph1)], axis=1)
    return bf(c2.reshape(128, N1 * 128)), bf(s2t.reshape(128, N1 * 128)), bf(t1a), bf(t1b)


def build(cfg, debug=None):
    TPC, NT, NTOK = cfg.TPC, cfg.NT, cfg.NTOK
    nc = bass.Bass("TRN2", target_bir_lowering=False)
    R = Rec(nc)

    def din(name, shape, dt=F32):
        return nc.dram_tensor(name, list(shape), dt, kind="ExternalInput").ap()

    def dscr(name, shape, dt):
        return nc.dram_tensor(name, list(shape), dt).ap()

    x_in = din("x", [TPC, D])
    cT = din("cT", [128, 8])
    w_ada = din("w_ada", [4, D, 6 * D])
    b_ada = din("b_ada", [4, 6 * D])
    norm_g = din("norm_g", [4, 2, D])
    a_w_in = din("a_w_in", [2, D, 2 * D])
    a_g_v = din("a_g_v", [2, D])
    a_w_sT = din("a_w_sT", [2, 8, 128, 128])
    a_b_sT = din("a_b_sT", [2, 128, 8])
    a_w_out = din("a_w_out", [2, D, D])
    b_w_qkv = din("b_w_qkv", [D, 1536])
    b_sinks = din("b_sinks", [1, 16])
    b_w_o = din("b_w_o", [D, D])
    c_w_inT = din("c_w_inT", [128, D])
    c_w_out = din("c_w_out", [D, D])
    f_w_up = din("f_w_up", [4, D, 2 * FF])
    f_w_convT = din("f_w_convT", [4, 128, 3, 44])
    f_b_convT = din("f_b_convT", [4, 128, 44])
    f_w_down = din("f_w_down", [4, FF, D])
    g_final = din("g_final", [1, D])
    ident_in = din("ident", [128, 128], BF16)
    m2_in = din("m2", [2, 1])
    mk_in = din("mk", [128, 2])
    mm_in = din("mm", [128, 2])
    adist = din("adist", [128, 384])
    amask = din("amask", [128, 384])
    cs128 = din("cs128", [128, 256], BF16)
    tabs = {}
    for (st, S) in cfg.seqs:
        if S not in tabs:
            N1 = S // 128
            tabs[S] = (din("t2c_%d" % S, [128, N1 * 128], BF16), din("t2s_%d" % S, [128, N1 * 128], BF16),
                       din("t1a_%d" % S, [N1, 2 * N1], BF16), din("t1b_%d" % S, [N1, 2 * N1], BF16))
    y_out = nc.dram_tensor("y", [TPC, D], F32, kind="ExternalOutput").ap()
    dbg_out = None
    if debug:
        dbg_out = nc.dram_tensor("dbg", [TPC, D], F32, kind="ExternalOutput").ap()

    xs = dscr("xs", [TPC, D], F32)
    modrow = dscr("modrow", [4, 6 * D], F32)
    hbuf = dscr("hbuf", [TPC + 2, D], BF16)
    agF_in = dscr("agF_in", [2, D], BF16)
    agF_out = dscr("agF_out", [2 * NCORE, D], BF16)
    cpF = dscr("cpF", [2 * NCORE + 4, D], BF16)
    kvbuf = dscr("kvbuf", [TPC + 256, 512], BF16)
    agK_in = dscr("agK_in", [256, 512], BF16)
    agK_out = dscr("agK_out", [256 * NCORE, 512], BF16)
    cpK = dscr("cpK", [256 * (NCORE + 2), 512], BF16)
    agH_in = dscr("agH_in", [TPC, D], BF16)
    agH_out = dscr("agH_out", [NTOK, D], BF16)
    zbuf = dscr("zbuf", [NTOK, 256], BF16)
    agY_in = dscr("agY_in", [NTOK, 128], BF16)
    agY_out = dscr("agY_out", [NCORE * NTOK, 128], BF16)
    cpY = dscr("cpY", [NCORE * NTOK, 128], BF16)

    fown = dscr("fown", [NCORE, TPC, 128], BF16)
    pid = nc.partition_id([mybir.EngineType.SP])
    pid_act = nc.partition_id([mybir.EngineType.Activation])

    ARENA = 192 * 1024
    WORK_END = ARENA - 15872
    arena = nc.alloc_sbuf_tensor("arena", [128, ARENA // 2], BF16)

    class Carver:
        def __init__(self, lo, hi):
            self.lo, self.hi, self.p = lo, hi, lo

        def get(self, shape, dt, name):
            esz = 4 if dt == F32 or dt == I32 else 2
            n = int(np.prod(shape[1:]))
            nb = (n * esz + 31) // 32 * 32
            assert self.p + nb <= self.hi, ("arena overflow", name, self.p, nb, self.hi)
            v = arena[0:shape[0], self.p // 2:(self.p + n * esz) // 2]
            if esz == 4:
                v = v.bitcast(dt)
            self.p += nb
            if len(shape) == 3:
                v = v.rearrange("p (a b) -> p a b", a=shape[1])
            elif len(shape) == 4:
                v = v.rearrange("p (a b c) -> p a b c", a=shape[1], b=shape[2])
            return v, Buf(name)

    WREG = 132 * 1024
    pers = Carver(WORK_END, ARENA)
    ident, b_ident = pers.get([128, 128], BF16, "ident")
    csb, b_csb = pers.get([128, 8, 128], BF16, "csb")
    ones32, b_ones = pers.get([128, 128], F32, "ones")
    modp, b_modp = pers.get([128, 3, D], F32, "modp")
    m2t, b_m2 = pers.get([2, 1], F32, "m2")
    mkt, b_mk = pers.get([128, 2], F32, "mk")
    mmt, b_mm = pers.get([128, 2], F32, "mm")
    small, b_small = pers.get([128, 64], F32, "small")

    pT = [nc.alloc_psum_tensor("pT%d" % i, [128, 1024], BF16).ap() for i in range(2)]
    bpT = [Buf("pT%d" % i) for i in range(2)]
    pM = [nc.alloc_psum_tensor("pM%d" % i, [128, 512], F32).ap() for i in range(6)]
    bpM = [Buf("pM%d" % i) for i in range(6)]
    rr = {"pT": 0, "pM": 0}

    def next_pT():
        i = rr["pT"] % 2
        rr["pT"] += 1
        return pT[i], bpT[i]

    def next_pM():
        i = rr["pM"] % 6
        rr["pM"] += 1
        return pM[i], bpM[i]

    R.dma("sp", ident, ident_in, wr=[b_ident], sb=b_ident)
    R.dma("sp", m2t, m2_in, wr=[b_m2], sb=b_m2)
    R.dma("sp", mkt, mk_in, wr=[b_mk], sb=b_mk)
    R.dma("sp", mmt, mm_in, wr=[b_mm], sb=b_mm)
    R.op("dve", lambda e: e.memset(ones32, 1.0), wr=[b_ones])
    wk = Carver(WREG, WORK_END)
    ct, b_ct = wk.get([128, 8], F32, "ct")
    cs, b_cs = wk.get([128, 8], F32, "cs")
    R.dma("sp", ct, cT, wr=[b_ct], sb=b_ct)
    R.op("act", lambda e: e.activation(out=cs, in_=ct, func=AF.Silu), rd=[b_ct], wr=[b_cs])
    for a in range(8):
        R.op("dve", lambda e, a=a: e.tensor_scalar(out=csb[:, a, :], in0=ones32, scalar1=cs[:, a:a + 1], scalar2=None,
                                                   op0=ALU.mult), rd=[b_ones, b_cs], wr=[b_csb])
    wv = Carver(0, WREG)
    wst = [wv.get([128, 2048], BF16, "wst%d" % i) for i in range(2)]
    mrow, b_mrow = wv.get([128, 2048], F32, "mrow")
    brow, b_brow = wv.get([128, 2048], F32, "brow")
    k = 0
    for l in range(4):
        for grp in range(3):
            cols = slice(grp * 2048, (grp + 1) * 2048)
            R.dma("sp", brow[0:1, :], b_ada[l:l + 1, cols], wr=[b_brow], sb=b_brow)
            ps = [next_pM() for _ in range(4)]
            for a in range(8):
                w, bw = wst[k % 2]
                k += 1
                R.dma("pool", w, w_ada[l, a * 128:(a + 1) * 128, cols], wr=[bw], sb=bw)
                for j in range(4):
                    R.op("pe", lambda e, j=j, a=a, w=w: e.matmul(ps[j][0], lhsT=csb[:, a, :], rhs=w[:, j * 512:(j + 1) * 512],
                                                                 start=(a == 0), stop=(a == 7)),
                         rd=[b_csb, bw], wr=[ps[j][1]], sig=(a == 7 or j == 3))
            for j in range(4):
                R.op("dve", lambda e, j=j: e.tensor_tensor(out=mrow[0:1, j * 512:(j + 1) * 512], in0=ps[j][0][0:1, :],
                                                           in1=brow[0:1, j * 512:(j + 1) * 512], op=ALU.add),
                     rd=[ps[j][1], b_brow], wr=[b_mrow])
            R.dma("sp", modrow[l:l + 1, cols], mrow[0:1, :], rd=[b_mrow], wr=[], sb=b_mrow)
    xt0, b_xt0 = wk.get([128, D], F32, "xt0")
    for q in range(NT):
        R.dma("sp", xt0, x_in[q * 128:(q + 1) * 128, :], wr=[b_xt0], sb=b_xt0)
        R.dma("sp", xs[q * 128:(q + 1) * 128, :], xt0, rd=[b_xt0], sb=b_xt0)
    R.barrier()

    def load_mod(l, which):
        base = which * 3 * D
        wkk = Carver(WREG, WORK_END)
        gt, b_gt = wkk.get([128, D], F32, "gtmp")
        R.dma("sp", modp.rearrange("p a b -> p (a b)"), modrow[l:l + 1, base:base + 3 * D].partition_broadcast(128),
              wr=[b_modp], sb=b_modp)
        R.dma("sp", gt, norm_g[l, which:which + 1, :].partition_broadcast(128), wr=[b_gt], sb=b_gt)
        R.op("dve", lambda e: e.scalar_tensor_tensor(out=modp[:, 1, :], in0=modp[:, 1, :], scalar=1.0, in1=gt,
                                                     op0=ALU.add, op1=ALU.mult), rd=[b_modp, b_gt], wr=[b_modp])
        R.barrier()

    class Work:
        pass

    def norm_tile(W, src_ap, slot, gA=None, gB=None):
        xt, bxt = W.xt[slot % len(W.xt)]
        R.dma("sp", xt, src_ap, wr=[bxt], sb=bxt)
        junk, bj = W.junk
        ss, bss = W.ss[slot % len(W.ss)]
        R.op("act", lambda e: e.activation(out=junk, in_=xt, func=AF.Square, accum_out=ss[:, 0:1]), rd=[bxt], wr=[bj, bss])
        R.op("dve", lambda e: e.tensor_scalar(out=ss[:, 1:2], in0=ss[:, 0:1], scalar1=1.0 / D, scalar2=EPS, op0=ALU.mult,
                                              op1=ALU.add), rd=[bss], wr=[bss])
        R.op("act", lambda e: e.activation(out=ss[:, 2:3], in_=ss[:, 1:2], func=AF.Sqrt), rd=[bss], wr=[bss])
        R.op("dve", lambda e: e.reciprocal(out=ss[:, 3:4], in_=ss[:, 2:3]), rd=[bss], wr=[bss])
        A = modp[:, 1, :] if gA is None else gA
        R.op("dve", lambda e: e.scalar_tensor_tensor(out=junk, in0=xt, scalar=ss[:, 3:4], in1=A, op0=ALU.mult, op1=ALU.mult),
             rd=[bxt, bss, b_modp], wr=[bj])
        hb, bhb = W.hb[slot % len(W.hb)]
        if gB is None:
            R.op("pool", lambda e: e.tensor_tensor(out=hb, in0=junk, in1=modp[:, 0, :], op=ALU.add), rd=[bj, b_modp], wr=[bhb])
        else:
            R.op("pool", lambda e: e.tensor_copy(out=hb, in_=junk), rd=[bj], wr=[bhb])
        return xt, bxt, hb, bhb

    def transpose_tile(src, bsrc, dst, bdst, nrows=128, ncols=128, dcol=None):
        p, bp = next_pT()
        for a in range(8):
            R.op("pe", lambda e, a=a: e.transpose(p[:, a * 128:a * 128 + nrows], src[0:nrows, a * 128:(a + 1) * 128],
                                                  ident[0:nrows, 0:nrows]),
                 rd=[bsrc, b_ident], wr=[bp], sig=(a == 7))
        pv = p.rearrange("p (a t) -> p a t", a=8)[:, :, 0:nrows]
        R.op("act", lambda e: e.activation(out=dst if dcol is None else dcol, in_=pv, func=AF.Copy), rd=[bp], wr=[bdst])

    def load_w(dst, bdst, src, eng="pool"):
        R.dma(eng, dst, src, wr=[bdst], sb=bdst)

    def proj_tm(lhsT, blhs, w, bw, ncol, K=8):
        outs = []
        for c0 in range(0, ncol, 512):
            cw = min(512, ncol - c0)
            p, bp = next_pM()
            for a in range(K):
                R.op("pe", lambda e, a=a, c0=c0, cw=cw, p=p: e.matmul(p[:, 0:cw], lhsT=lhsT[:, a, :], rhs=w[:, a, c0:c0 + cw],
                                                                    start=(a == 0), stop=(a == K - 1)),
                     rd=[blhs, bw], wr=[bp], sig=(a == K - 1))
            outs.append((p, bp, cw, c0))
        return outs

    def residual_out(W, outs, xt, bxt, q, final_dst=None):
        tmp, btmp = W.junk
        for (p, bp, cw, c0) in outs:
            R.op("dve", lambda e, p=p, c0=c0, cw=cw: e.tensor_tensor(out=tmp[:, c0:c0 + cw], in0=p[:, 0:cw],
                                                                      in1=modp[:, 2, c0:c0 + cw], op=ALU.mult),
                 rd=[bp, b_modp], wr=[btmp])
        R.op("pool", lambda e: e.tensor_tensor(out=xt, in0=xt, in1=tmp, op=ALU.add), rd=[btmp, bxt], wr=[bxt])
        R.dma("sp", xs[q * 128:(q + 1) * 128, :], xt, rd=[bxt], sb=bxt)

    def std_work(cv):
        W = Work()
        W.xt = [cv.get([128, D], F32, "xt%d" % i) for i in range(2)]
        W.junk = cv.get([128, D], F32, "junk")
        W.ss = [cv.get([128, 4], F32, "ss%d" % i) for i in range(2)]
        W.hb = [cv.get([128, D], BF16, "hb%d" % i) for i in range(2)]
        W.hT = [cv.get([128, 8, 128], BF16, "hT%d" % i) for i in range(2)]
        return W

    def phase_gmlp(l, j):
        load_mod(l, 0)
        wv = Carver(0, WREG)
        w_in, b_win = wv.get([128, 8, 2 * D], BF16, "a_w_in")
        w_out, b_wout = wv.get([128, 8, D], BF16, "a_w_out")
        w_s, b_ws = wv.get([128, 8, 128], BF16, "a_w_s")
        gv, b_gv = wv.get([128, D], F32, "a_g_v")
        bsT, b_bsT = wv.get([128, 8], F32, "a_b_sT")
        us = [wv.get([128, D], F32, "u%d" % i) for i in range(2)]
        vs_ = [wv.get([128, D], F32, "v%d" % i) for i in range(2)]
        vns = [wv.get([128, D], BF16, "vn%d" % i) for i in range(2)]
        gbs = [wv.get([128, D], BF16, "gated%d" % i) for i in range(2)]
        gTs = [wv.get([128, 8, 128], BF16, "gT%d" % i) for i in range(2)]
        vss = [wv.get([128, 8], F32, "vs%d" % i) for i in range(2)]
        for a in range(8):
            load_w(w_in[:, a, :], b_win, a_w_in[j, a * 128:(a + 1) * 128, :])
            load_w(w_out[:, a, :], b_wout, a_w_out[j, a * 128:(a + 1) * 128, :])
            load_w(w_s[:, a, :], b_ws, a_w_sT[j, a, :, :])
        R.dma("sp", gv, a_g_v[j:j + 1, :].partition_broadcast(128), wr=[b_gv], sb=b_gv)
        R.dma("sp", bsT, a_b_sT[j, :, :], wr=[b_bsT], sb=b_bsT)
        W = std_work(Carver(WREG, WORK_END))
        for q in range(NT):
            u, b_u = us[q % 2]
            v, b_v = vs_[q % 2]
            vn, b_vn = vns[q % 2]
            gb, b_gb = gbs[q % 2]
            gT, b_gT = gTs[q % 2]
            vs, b_vs = vss[q % 2]
            xt, bxt, hb, bhb = norm_tile(W, xs[q * 128:(q + 1) * 128, :], q)
            hT, bhT = W.hT[q % 2]
            transpose_tile(hb, bhb, hT, bhT)
            outs = proj_tm(hT, bhT, w_in, b_win, 2 * D)
            for i, (p, bp, cw, c0) in enumerate(outs):
                dst, bd = (u, b_u) if i < 2 else (v, b_v)
                cc = c0 % D
                R.op("act", lambda e, p=p, dst=dst, cc=cc: e.activation(out=dst[:, cc:cc + 512], in_=p, func=AF.Gelu),
                     rd=[bp], wr=[bd])
            R.op("act", lambda e: e.activation(out=vn, in_=v, func=AF.Square, accum_out=vs[:, 0:1]), rd=[b_v], wr=[b_vn, b_vs])
            R.op("dve", lambda e: e.tensor_scalar(out=vs[:, 1:2], in0=vs[:, 0:1], scalar1=1.0 / D, scalar2=EPS, op0=ALU.mult,
                                                  op1=ALU.add), rd=[b_vs], wr=[b_vs])
            R.op("act", lambda e: e.activation(out=vs[:, 2:3], in_=vs[:, 1:2], func=AF.Sqrt), rd=[b_vs], wr=[b_vs])
            R.op("dve", lambda e: e.reciprocal(out=vs[:, 3:4], in_=vs[:, 2:3]), rd=[b_vs], wr=[b_vs])
            R.op("dve", lambda e: e.scalar_tensor_tensor(out=vn, in0=v, scalar=vs[:, 3:4], in1=gv, op0=ALU.mult, op1=ALU.mult),
                 rd=[b_v, b_vs, b_gv], wr=[b_vn])
            for half in range(2):
                p, bp = next_pM()
                for gg in range(4):
                    g = half * 4 + gg
                    R.op("pe", lambda e, g=g, gg=gg, p=p: e.matmul(p[:, gg * 128:(gg + 1) * 128], lhsT=w_s[:, g, :],
                                                                  rhs=vn[:, g * 128:(g + 1) * 128], start=True, stop=True),
                         rd=[b_ws, b_vn], wr=[bp], sig=(gg == 3))
                for gg in range(4):
                    g = half * 4 + gg
                    R.op("dve", lambda e, g=g, gg=gg, p=p: e.scalar_tensor_tensor(
                        out=gb[:, g * 128:(g + 1) * 128], in0=p[:, gg * 128:(gg + 1) * 128], scalar=bsT[:, g:g + 1],
                        in1=u[:, g * 128:(g + 1) * 128], op0=ALU.add, op1=ALU.mult), rd=[bp, b_bsT, b_u], wr=[b_gb])
            transpose_tile(gb, b_gb, gT, b_gT)
            outs = proj_tm(gT, b_gT, w_out, b_wout, D)
            residual_out(W, outs, xt, bxt, q)
        R.barrier()

    def phase_ffn(l):
        TF = cfg.TF
        load_mod(l, 1)
        wv = Carver(0, WREG)
        w_up, b_wup = wv.get([128, 8, 2 * FF], BF16, "w_up")
        w_dn, b_wdn = wv.get([128, 22, D], BF16, "w_dn")
        for a in range(8):
            load_w(w_up[:, a, :], b_wup, f_w_up[l, a * 128:(a + 1) * 128, :])
        for jj in range(22):
            load_w(w_dn[:, jj, :], b_wdn, f_w_down[l, jj * 128:(jj + 1) * 128, :])
        cv = Carver(WREG, WORK_END)
        W = std_work(cv)
        hh, b_hh = cv.get([2, D], BF16, "hh")
        R.op("dve", lambda e: e.memset(hh, 0.0), wr=[b_hh])
        R.dma("sp", cpF[0:2, :], hh, rd=[b_hh], sb=b_hh)
        R.dma("sp", cpF[2 + 2 * NCORE:4 + 2 * NCORE, :], hh, rd=[b_hh], sb=b_hh)
        for q in range(NT):
            xt, bxt, hb, bhb = norm_tile(W, xs[q * 128:(q + 1) * 128, :], q)
            R.dma("sp", hbuf[1 + q * 128:1 + (q + 1) * 128, :], hb, rd=[bhb], sb=bhb)
            if q == 0:
                R.dma("sp", agF_in[0:1, :], hb[0:1, :], rd=[bhb], sb=bhb)
            if q == NT - 1:
                R.dma("sp", agF_in[1:2, :], hb[127:128, :], rd=[bhb], sb=bhb)
        R.barrier()
        b_ag = Buf("agF")
        R.allgather(agF_in, agF_out, sb=b_ag)
        R.barrier()
        R.dma("sp", cpF[2:2 + 2 * NCORE, :], agF_out, sb=Buf("cpF"))
        R.barrier()
        R.dma("sp", hh[0:2, :], cpF[bass.ds(pid * 2 + 1, 4), :][0:4:3, :], wr=[b_hh], sb=b_hh)
        R.op("dve", lambda e: e.tensor_scalar(out=hh, in0=hh, scalar1=m2t[:, 0:1], scalar2=None, op0=ALU.mult),
             rd=[b_hh, b_m2], wr=[b_hh])
        R.dma("sp", hbuf[0:1, :], hh[0:1, :], rd=[b_hh], sb=b_hh)
        R.dma("sp", hbuf[TPC + 1:TPC + 2, :], hh[1:2, :], rd=[b_hh], sb=b_hh)
        R.barrier()
        cv = Carver(WREG, WORK_END)
        xts = [cv.get([128, D], F32, "fxt%d" % i) for i in range(2)]
        tmpo = [cv.get([128, D], F32, "ftmp%d" % i) for i in range(1)]
        hms = [cv.get([128, D], BF16, "hm%d" % i) for i in range(2)]
        hX = [cv.get([128, 8, TF + 2], BF16, "hX%d" % i) for i in range(2)]
        actT, b_actT = cv.get([128, 22, TF], BF16, "actT")
        cas = [cv.get([128, TF], F32, "ca%d" % i) for i in range(2)]
        cgs = [cv.get([128, TF], F32, "cg%d" % i) for i in range(2)]
        sgs = [cv.get([128, TF], F32, "sg%d" % i) for i in range(2)]
        wc, b_wc = cv.get([128, 3, 44], F32, "wconv")
        bc, b_bc = cv.get([128, 44], F32, "bconv")
        R.dma("sp", wc, f_w_convT[l], wr=[b_wc], sb=b_wc)
        R.dma("sp", bc, f_b_convT[l], wr=[b_bc], sb=b_bc)
        tiles = []
        t0 = 0
        while t0 < TPC:
            n = min(TF, TPC - t0)
            tiles.append((t0, n))
            t0 += n
        cnt = {"hm": 0, "x": 0, "c": 0}

        def prep(i):
            t0, n = tiles[i]
            hx, bhx = hX[i % 2]
            tot = n + 2
            for s0 in range(0, tot, 128):
                nr = min(128, tot - s0)
                hm, bhm = hms[cnt["hm"] % 2]
                cnt["hm"] += 1
                R.dma("sp", hm[0:nr, :], hbuf[t0 + s0:t0 + s0 + nr, :], wr=[bhm], sb=bhm)
                transpose_tile(hm, bhm, hx, bhx, nrows=nr, dcol=hx[:, :, s0:s0 + nr])

        def up(i):
            t0, n = tiles[i]
            hx, bhx = hX[i % 2]
            for jj in range(22):
                k = cnt["c"] % 2
                cnt["c"] += 1
                ca, b_ca = cas[k]
                cg, b_cg = cgs[k]
                sg, b_sg = sgs[k]
                for part in range(2):
                    ch = jj + 22 * part
                    p, bp = next_pM()
                    for a in range(8):
                        R.op("pe", lambda e, a=a, ch=ch, p=p: e.matmul(p[:, 0:n + 2], lhsT=w_up[:, a, ch * 128:(ch + 1) * 128],
                                                                      rhs=hx[:, a, 0:n + 2], start=(a == 0), stop=(a == 7)),
                             rd=[b_wup, bhx], wr=[bp], sig=(a == 7))
                    dst, bd = (ca, b_ca) if part == 0 else (cg, b_cg)
                    R.op("act", lambda e, p=p, ch=ch, dst=dst: e.activation(out=dst[:, 0:n], in_=p[:, 1:n + 1], func=AF.Identity,
                                                                          bias=bc[:, ch:ch + 1], scale=wc[:, 1, ch:ch + 1]),
                         rd=[bp, b_wc, b_bc], wr=[bd])
                    R.op("dve", lambda e, p=p, ch=ch, dst=dst: e.scalar_tensor_tensor(
                        out=dst[:, 0:n], in0=p[:, 0:n], scalar=wc[:, 0, ch:ch + 1], in1=dst[:, 0:n], op0=ALU.mult, op1=ALU.add),
                        rd=[bp, b_wc, bd], wr=[bd])
                    R.op("dve", lambda e, p=p, ch=ch, dst=dst: e.scalar_tensor_tensor(
                        out=dst[:, 0:n], in0=p[:, 2:n + 2], scalar=wc[:, 2, ch:ch + 1], in1=dst[:, 0:n], op0=ALU.mult, op1=ALU.add),
                        rd=[bp, b_wc, bd], wr=[bd])
                R.op("act", lambda e, sg=sg, cg=cg: e.activation(out=sg[:, 0:n], in_=cg[:, 0:n], func=AF.Silu), rd=[b_cg], wr=[b_sg])
                R.op("pool", lambda e, jj=jj, ca=ca, sg=sg: e.tensor_tensor(out=actT[:, jj, 0:n], in0=ca[:, 0:n], in1=sg[:, 0:n],
                                                                            op=ALU.mult), rd=[b_ca, b_sg], wr=[b_actT])

        def down(i):
            t0, n = tiles[i]
            for s0 in range(0, n, 128):
                m = min(128, n - s0)
                xt, bxt = xts[cnt["x"] % 2]
                cnt["x"] += 1
                tmp, btmp = tmpo[0]
                R.dma("sp", xt[0:m, :], xs[t0 + s0:t0 + s0 + m, :], wr=[bxt], sb=bxt)
                for c0 in (0, 512):
                    p, bp = next_pM()
                    for jj in range(22):
                        R.op("pe", lambda e, jj=jj, p=p, c0=c0: e.matmul(p[0:m, :], lhsT=actT[:, jj, s0:s0 + m], rhs=w_dn[:, jj, c0:c0 + 512],
                                                                        start=(jj == 0), stop=(jj == 21)),
                             rd=[b_actT, b_wdn], wr=[bp], sig=(jj == 21))
                    R.op("dve", lambda e, p=p, c0=c0: e.tensor_tensor(out=tmp[0:m, c0:c0 + 512], in0=p[0:m, :], in1=modp[0:m, 2, c0:c0 + 512],
                                                                      op=ALU.mult), rd=[bp, b_modp], wr=[btmp])
                R.op("pool", lambda e, xt=xt, tmp=tmp: e.tensor_tensor(out=xt[0:m, :], in0=xt[0:m, :], in1=tmp[0:m, :], op=ALU.add),
                     rd=[btmp, bxt], wr=[bxt])
                R.dma("sp", xs[t0 + s0:t0 + s0 + m, :], xt[0:m, :], rd=[bxt], sb=bxt)

        prep(0)
        for i in range(len(tiles)):
            up(i)
            if i + 1 < len(tiles):
                prep(i + 1)
            down(i)
        R.barrier()

    def phase_attn(l):
        load_mod(l, 0)
        wv = Carver(0, WREG)
        w_qkv, b_wq = wv.get([128, 8, 1536], BF16, "w_qkv")
        w_o, b_wo = wv.get([128, 8, D], BF16, "w_o")
        bias, b_bias = wv.get([128, 16, 384], F32, "bias")
        dist, b_dist = wv.get([128, 384], F32, "dist")
        amk, b_amk = wv.get([128, 384], F32, "amk")
        snk, b_snk = wv.get([128, 16], F32, "snk")
        kvt = [wv.get([128, 512], BF16, "kvt%d" % i) for i in range(2)]
        kh, b_kh = wv.get([128, 2, 512], BF16, "kh")
        zt, b_zt = wv.get([128, 512], BF16, "zt")
        kb = [wv.get([128, 3, 512], BF16, "kb%d" % i) for i in range(2)]
        kT, b_kT = wv.get([128, 2, 384], BF16, "kT")
        qb, b_qb = wv.get([128, D], BF16, "qb")
        qT, b_qT = wv.get([128, 8, 128], BF16, "qT")
        sc = [wv.get([128, 384], F32, "sc%d" % i) for i in range(2)]
        eb = [wv.get([128, 384], BF16, "eb%d" % i) for i in range(2)]
        ebT = [wv.get([128, 3, 128], BF16, "ebT%d" % i) for i in range(2)]
        st = [wv.get([128, 8], F32, "st%d" % i) for i in range(2)]
        ob, b_ob = wv.get([128, D], BF16, "ob")
        oT, b_oT = wv.get([128, 8, 128], BF16, "oT")
        for a in range(8):
            load_w(w_qkv[:, a, :], b_wq, b_w_qkv[a * 128:(a + 1) * 128, :])
            load_w(w_o[:, a, :], b_wo, b_w_o[a * 128:(a + 1) * 128, :])
        R.dma("sp", dist, adist, wr=[b_dist], sb=b_dist)
        R.dma("sp", amk, amask, wr=[b_amk], sb=b_amk)
        R.dma("sp", snk, b_sinks[0:1, :].partition_broadcast(128), wr=[b_snk], sb=b_snk)
        for h in range(16):
            slope = float(2.0 ** (-8.0 * (h + 1) / 16))
            R.op("dve", lambda e, h=h, slope=slope: e.scalar_tensor_tensor(out=bias[:, h, :], in0=dist, scalar=-slope, in1=amk,
                                                                           op0=ALU.mult, op1=ALU.add),
                 rd=[b_dist, b_amk], wr=[b_bias])
        W = std_work(Carver(WREG, WORK_END))
        R.op("dve", lambda e: e.memset(zt, 0.0), wr=[b_zt])
        for blk in (0, NCORE + 1):
            for hf in range(2):
                R.dma("sp", cpK[blk * 256 + hf * 128:blk * 256 + (hf + 1) * 128, :], zt, rd=[b_zt], sb=b_zt)
        for q in range(NT):
            xt, bxt, hb, bhb = norm_tile(W, xs[q * 128:(q + 1) * 128, :], q)
            hT, bhT = W.hT[q % 2]
            transpose_tile(hb, bhb, hT, bhT)
            p, bp = next_pM()
            for a in range(8):
                R.op("pe", lambda e, a=a, p=p: e.matmul(p, lhsT=hT[:, a, :], rhs=w_qkv[:, a, 1024:1536], start=(a == 0), stop=(a == 7)),
                     rd=[bhT, b_wq], wr=[bp], sig=(a == 7))
            kv, bkv = kvt[q % 2]
            R.op("act", lambda e, p=p, kv=kv: e.activation(out=kv, in_=p, func=AF.Copy), rd=[bp], wr=[bkv])
            R.dma("sp", kvbuf[128 + q * 128:128 + (q + 1) * 128, :], kv, rd=[bkv], sb=bkv)
            if q == 0:
                R.dma("sp", agK_in[0:128, :], kv, rd=[bkv], sb=bkv)
            if q == NT - 1:
                R.dma("sp", agK_in[128:256, :], kv, rd=[bkv], sb=bkv)
        R.barrier()
        b_ag = Buf("agK")
        R.allgather(agK_in, agK_out, sb=b_ag)
        R.barrier()
        R.dma("sp", cpK[256:256 + 256 * NCORE, :], agK_out, sb=Buf("cpK"))
        R.barrier()
        R.dma("sp", kh, cpK.rearrange("(b p) c -> p b c", p=128)[:, bass.ds(pid * 2 + 1, 4), :][:, 0:4:3, :], wr=[b_kh], sb=b_kh)
        for sd in range(2):
            R.op("dve", lambda e, sd=sd: e.tensor_scalar(out=kh[:, sd, :], in0=kh[:, sd, :], scalar1=mmt[:, sd:sd + 1], scalar2=None,
                                                         op0=ALU.mult), rd=[b_kh, b_mm], wr=[b_kh])
        R.dma("sp", kvbuf[0:128, :], kh[:, 0, :], rd=[b_kh], sb=b_kh)
        R.dma("sp", kvbuf[TPC + 128:TPC + 256, :], kh[:, 1, :], rd=[b_kh], sb=b_kh)
        R.barrier()
        for q in range(NT):
            xt, bxt, hb, bhb = norm_tile(W, xs[q * 128:(q + 1) * 128, :], q)
            hT, bhT = W.hT[q % 2]
            transpose_tile(hb, bhb, hT, bhT)
            kbt, bkb = kb[q % 2]
            R.dma("sp", kbt, kvbuf[q * 128:q * 128 + 384, :].rearrange("(b p) c -> p b c", p=128), wr=[bkb], sb=bkb)
            for g in range(2):
                p, bp = next_pM()
                for a in range(8):
                    R.op("pe", lambda e, a=a, p=p, g=g: e.matmul(p, lhsT=hT[:, a, :], rhs=w_qkv[:, a, g * 512:(g + 1) * 512],
                                                                start=(a == 0), stop=(a == 7)),
                         rd=[bhT, b_wq], wr=[bp], sig=(a == 7))
                R.op("act", lambda e, p=p, g=g: e.activation(
                    out=qb[:, g * 512:(g + 1) * 512].rearrange("p (i k d) -> p k i d", i=4, k=2, d=64),
                    in_=p.rearrange("p (k i d) -> p k i d", k=2, i=4, d=64), func=AF.Copy), rd=[bp], wr=[b_qb])
            transpose_tile(qb, b_qb, qT, b_qT)
            p, bp = next_pT()
            for c2 in range(2):
                for b in range(3):
                    R.op("pe", lambda e, c2=c2, b=b, p=p: e.transpose(p[:, (c2 * 3 + b) * 128:(c2 * 3 + b + 1) * 128],
                                                                     kbt[:, b, c2 * 128:(c2 + 1) * 128], ident),
                         rd=[bkb, b_ident], wr=[bp], sig=(c2 == 1 and b == 2))
            R.op("act", lambda e, p=p: e.activation(out=kT.rearrange("p c k -> p (c k)"), in_=p[:, 0:768], func=AF.Copy),
                 rd=[bp], wr=[b_kT])
            po = [next_pM(), next_pM()]
            for h in range(16):
                g, k, i = h // 8, (h // 4) % 2, h % 4
                kvh = h // 4
                s_, bs_ = sc[h % 2]
                e_, be_ = eb[h % 2]
                eT, beT = ebT[h % 2]
                t_, bt_ = st[h % 2]
                p, bp = next_pM()
                R.op("pe", lambda e, p=p, g=g, k=k, i=i: e.matmul(p[:, 0:384], lhsT=qT[k * 64:(k + 1) * 64, g * 4 + i, :],
                                                                 rhs=kT[k * 64:(k + 1) * 64, g, :], start=True, stop=True),
                     rd=[b_qT, b_kT], wr=[bp])
                R.op("dve", lambda e, p=p, h=h, s_=s_: e.scalar_tensor_tensor(out=s_, in0=p[:, 0:384], scalar=0.125, in1=bias[:, h, :],
                                                                              op0=ALU.mult, op1=ALU.add),
                     rd=[bp, b_bias], wr=[bs_])
                if q == 0:
                    R.op("dve", lambda e, s_=s_: e.tensor_scalar(out=s_[:, 0:128], in0=s_[:, 0:128], scalar1=mkt[:, 0:1], scalar2=None,
                                                                 op0=ALU.add), rd=[bs_, b_mk], wr=[bs_])
                if q == NT - 1:
                    R.op("dve", lambda e, s_=s_: e.tensor_scalar(out=s_[:, 256:384], in0=s_[:, 256:384], scalar1=mkt[:, 1:2],
                                                                 scalar2=None, op0=ALU.add), rd=[bs_, b_mk], wr=[bs_])
                R.op("dve", lambda e, s_=s_, t_=t_: e.reduce_max(out=t_[:, 0:1], in_=s_, axis=AX.X), rd=[bs_], wr=[bt_])
                R.op("dve", lambda e, t_=t_, h=h: e.tensor_tensor(out=t_[:, 1:2], in0=t_[:, 0:1], in1=snk[:, h:h + 1], op=ALU.max),
                     rd=[bt_, b_snk], wr=[bt_])
                R.op("dve", lambda e, t_=t_: e.tensor_scalar(out=t_[:, 2:3], in0=t_[:, 1:2], scalar1=-1.0, scalar2=None, op0=ALU.mult),
                     rd=[bt_], wr=[bt_])
                R.op("act", lambda e, s_=s_, e_=e_, t_=t_: e.activation(out=e_, in_=s_, func=AF.Exp, bias=t_[:, 2:3], scale=1.0,
                                                                        accum_out=t_[:, 3:4]), rd=[bs_, bt_], wr=[be_, bt_])
                R.op("act", lambda e, t_=t_, h=h: e.activation(out=t_[:, 4:5], in_=snk[:, h:h + 1], func=AF.Exp, bias=t_[:, 2:3],
                                                               scale=1.0), rd=[b_snk, bt_], wr=[bt_])
                R.op("dve", lambda e, t_=t_: e.tensor_tensor(out=t_[:, 5:6], in0=t_[:, 3:4], in1=t_[:, 4:5], op=ALU.add),
                     rd=[bt_], wr=[bt_])
                R.op("dve", lambda e, t_=t_: e.reciprocal(out=t_[:, 6:7], in_=t_[:, 5:6]), rd=[bt_], wr=[bt_])
                pt, bpt = next_pT()
                for b in range(3):
                    R.op("pe", lambda e, b=b, pt=pt, e_=e_: e.transpose(pt[:, b * 128:(b + 1) * 128], e_[:, b * 128:(b + 1) * 128], ident),
                         rd=[be_, b_ident], wr=[bpt], sig=(b == 2))
                R.op("act", lambda e, pt=pt, eT=eT: e.activation(out=eT.rearrange("p b k -> p (b k)"), in_=pt[:, 0:384], func=AF.Copy),
                     rd=[bpt], wr=[beT])
                pp, bpp = po[h // 8]
                sl = slice((h % 8) * 64, (h % 8 + 1) * 64)
                for b in range(3):
                    R.op("pe", lambda e, b=b, pp=pp, sl=sl, eT=eT, kvh=kvh: e.matmul(pp[:, sl], lhsT=eT[:, b, :],
                                                                                     rhs=kbt[:, b, 256 + kvh * 64:256 + (kvh + 1) * 64],
                                                                                     start=(b == 0), stop=(b == 2)),
                         rd=[beT, bkb], wr=[bpp], sig=(b == 2))
                R.op("dve", lambda e, pp=pp, sl=sl, h=h, t_=t_: e.tensor_scalar(out=ob[:, h * 64:(h + 1) * 64], in0=pp[:, sl],
                                                                                scalar1=t_[:, 6:7], scalar2=None, op0=ALU.mult),
                     rd=[bpp, bt_], wr=[b_ob])
            transpose_tile(ob, b_ob, oT, b_oT)
            outs = proj_tm(oT, b_oT, w_o, b_wo, D)
            residual_out(W, outs, xt, bxt, q)
        R.barrier()

    def phase_fft(l):
        load_mod(l, 0)
        W = std_work(Carver(WREG, WORK_END))
        for q in range(NT):
            xt, bxt, hb, bhb = norm_tile(W, xs[q * 128:(q + 1) * 128, :], q)
            hT, bhT = W.hT[q % 2]
            transpose_tile(hb, bhb, hT, bhT)
            R.dma("sp", agH_in[q * 128:(q + 1) * 128, :], hT.rearrange("p a t -> p (a t)"), rd=[bhT], sb=bhT)
        R.barrier()
        b_ag = Buf("agH")
        R.allgather(agH_in, agH_out, sb=b_ag)
        R.barrier()
        wv = Carver(0, WREG)
        wgT, b_wgT = wv.get([128, D], BF16, "wgT")
        csm, b_csm = wv.get([128, 256], BF16, "csm")
        wz, b_wz = wv.get([128, 8, 256], BF16, "wz")
        hTt = [wv.get([128, 8, 128], BF16, "hTt%d" % i) for i in range(3)]
        ztl = [wv.get([128, 256], BF16, "ztl%d" % i) for i in range(3)]
        load_w(wgT, b_wgT, c_w_inT)
        R.dma("sp", csm, cs128, wr=[b_csm], sb=b_csm)
        for a in range(8):
            p, bp = next_pM()
            R.op("pe", lambda e, a=a, p=p: e.matmul(p[:, 0:256], lhsT=wgT[:, a * 128:(a + 1) * 128], rhs=csm, start=True, stop=True),
                 rd=[b_wgT, b_csm], wr=[bp])
            R.op("act", lambda e, a=a, p=p: e.activation(out=wz[:, a, :], in_=p[:, 0:256], func=AF.Copy), rd=[bp], wr=[b_wz])
        for gq in range(NTOK // 128):
            ht, bht = hTt[gq % 3]
            R.dma("sp", ht.rearrange("p a t -> p (a t)"), agH_out[gq * 128:(gq + 1) * 128, :], wr=[bht], sb=bht)
            p, bp = next_pM()
            for a in range(8):
                R.op("pe", lambda e, a=a, p=p, ht=ht: e.matmul(p[:, 0:256], lhsT=ht[:, a, :], rhs=wz[:, a, :], start=(a == 0), stop=(a == 7)),
                     rd=[bht, b_wz], wr=[bp], sig=(a == 7))
            z_, bz_ = ztl[gq % 3]
            R.op("act", lambda e, p=p, z_=z_: e.activation(out=z_, in_=p[:, 0:256], func=AF.Copy), rd=[bp], wr=[bz_])
            R.dma("sp", zbuf[gq * 128:(gq + 1) * 128, :], z_, rd=[bz_], sb=bz_)
        R.barrier()
        for (st0, S) in cfg.seqs:
            N1 = S // 128
            t2c_d, t2s_d, t1a_d, t1b_d = tabs[S]
            fv = Carver(0, WORK_END)
            zz, b_zz = fv.get([128, 128, 256], BF16, "zz")
            U, b_U = fv.get([128, N1, 2, 128], BF16, "U")
            t1a, b_t1a = fv.get([128, 2 * N1], BF16, "t1a")
            t1b, b_t1b = fv.get([128, 2 * N1], BF16, "t1b")
            KC = min(16, N1)
            tc_ = [fv.get([128, KC * 128], BF16, "tc%d" % i) for i in range(2)]
            ts_ = [fv.get([128, KC * 128], BF16, "ts%d" % i) for i in range(2)]
            ys = [fv.get([128, KC, 128], BF16, "ys%d" % i) for i in range(2)]
            R.dma("sp", zz[0:N1].rearrange("p s c -> p (s c)"),
                  zbuf[st0:st0 + S, :].rearrange("(s1 s2) c -> s1 (s2 c)", s2=128), wr=[b_zz], sb=b_zz)
            R.dma("sp", t1a[0:N1, :], t1a_d, wr=[b_t1a], sb=b_t1a)
            R.dma("sp", t1b[0:N1, :], t1b_d, wr=[b_t1b], sb=b_t1b)
            cpb = 512 // (2 * N1)
            for c0 in range(0, 128, cpb):
                p, bp = next_pM()
                for ci in range(cpb):
                    c = c0 + ci
                    o_ = p[:, ci * 2 * N1:(ci + 1) * 2 * N1]
                    R.op("pe", lambda e, o_=o_, c=c: e.matmul(o_, lhsT=zz[0:N1, :, c], rhs=t1a[0:N1, :], start=True, stop=False),
                         rd=[b_zz, b_t1a], wr=[bp], sig=False)
                    R.op("pe", lambda e, o_=o_, c=c: e.matmul(o_, lhsT=zz[0:N1, :, 128 + c], rhs=t1b[0:N1, :], start=False, stop=True),
                         rd=[b_zz, b_t1b], wr=[bp], sig=(ci == cpb - 1))
                R.op("act", lambda e, p=p, c0=c0: e.activation(
                    out=U[:, :, :, c0:c0 + cpb].rearrange("p k r c -> p c r k"),
                    in_=p[:, 0:cpb * 2 * N1].rearrange("p (c r k) -> p c r k", c=cpb, r=2), func=AF.Copy), rd=[bp], wr=[b_U])
            scale = float(1.0 / np.sqrt(S * 128.0))
            ydst = agY_in[st0:st0 + S, :].rearrange("(k2 k1) c -> k2 k1 c", k1=N1)
            for kc in range(0, N1, KC):
                tcc, btc = tc_[(kc // KC) % 2]
                tss, bts = ts_[(kc // KC) % 2]
                yy, byy = ys[(kc // KC) % 2]
                R.dma("sp", tcc, t2c_d[:, kc * 128:(kc + KC) * 128], wr=[btc], sb=btc)
                R.dma("sp", tss, t2s_d[:, kc * 128:(kc + KC) * 128], wr=[bts], sb=bts)
                for k4 in range(0, KC, 4):
                    p, bp = next_pM()
                    nk = min(4, KC - k4)
                    for kk in range(nk):
                        k1 = kc + k4 + kk
                        o_ = p[:, kk * 128:(kk + 1) * 128]
                        R.op("pe", lambda e, o_=o_, k1=k1, kk=kk, k4=k4, tcc=tcc: e.matmul(
                            o_, lhsT=tcc[:, (k4 + kk) * 128:(k4 + kk + 1) * 128], rhs=U[:, k1, 0, :], start=True, stop=False),
                            rd=[btc, b_U], wr=[bp], sig=False)
                        R.op("pe", lambda e, o_=o_, k1=k1, kk=kk, k4=k4, tss=tss: e.matmul(
                            o_, lhsT=tss[:, (k4 + kk) * 128:(k4 + kk + 1) * 128], rhs=U[:, k1, 1, :], start=False, stop=True),
                            rd=[bts, b_U], wr=[bp], sig=(kk == nk - 1))
                    R.op("act", lambda e, p=p, k4=k4, nk=nk, yy=yy: e.activation(
                        out=yy[:, k4:k4 + nk, :].rearrange("p k c -> p (k c)"), in_=p[:, 0:nk * 128], func=AF.Copy, scale=scale),
                        rd=[bp], wr=[byy])
                R.dma("sp", ydst[:, kc:kc + KC, :], yy, rd=[byy], sb=byy)
            R.barrier()
        b_agy = Buf("agY")
        R.allgather(agY_in, agY_out, sb=b_agy)
        R.barrier()
        nrow = NCORE * NTOK
        step = nrow // 8
        b_cpy = Buf("cpY")
        for i in range(8):
            R.dma("sp", cpY[i * step:(i + 1) * step, :], agY_out[i * step:(i + 1) * step, :], sb=b_cpy)
        R.barrier()
        wv = Carver(0, WREG)
        w_out, b_wout = wv.get([128, 8, D], BF16, "c_w_out")
        ft = [wv.get([128, 8, 128], BF16, "ft%d" % i) for i in range(2)]
        fT, b_fT = wv.get([128, 8, 128], BF16, "fT")
        for a in range(8):
            load_w(w_out[:, a, :], b_wout, c_w_out[a * 128:(a + 1) * 128, :])
        cpYv = cpY.rearrange("(g t) c -> g t c", g=NCORE)
        b_fown = Buf("fown")
        for g in range(NCORE):
            R.dma("act", fown[g], cpYv[g, bass.ds(pid_act * TPC, TPC), :], sb=b_fown)
        R.barrier()
        for q in range(NT):
            f_, bf_ = ft[q % 2]
            R.dma("sp", f_, fown[:, q * 128:(q + 1) * 128, :].rearrange("g t c -> t g c"), wr=[bf_], sb=bf_)
            xt, bxt = W.xt[q % 2]
            R.dma("sp", xt, xs[q * 128:(q + 1) * 128, :], wr=[bxt], sb=bxt)
            transpose_tile(f_.rearrange("p g c -> p (g c)"), bf_, fT, b_fT)
            outs = proj_tm(fT, b_fT, w_out, b_wout, D)
            residual_out(W, outs, xt, bxt, q)
        R.barrier()

    def final_phase():
        wkk = Carver(WREG, WORK_END)
        W = std_work(wkk)
        gf, b_gf = Carver(0, WREG).get([128, D], F32, "gf")
        R.dma("sp", gf, g_final[0:1, :].partition_broadcast(128), wr=[b_gf], sb=b_gf)
        for q in range(NT):
            xt, bxt = W.xt[q % 2]
            R.dma("sp", xt, xs[q * 128:(q + 1) * 128, :], wr=[bxt], sb=bxt)
            junk, bj = W.junk
            ss, bss = W.ss[q % 2]
            R.op("act", lambda e: e.activation(out=junk, in_=xt, func=AF.Square, accum_out=ss[:, 0:1]), rd=[bxt], wr=[bj, bss])
            R.op("dve", lambda e: e.tensor_scalar(out=ss[:, 1:2], in0=ss[:, 0:1], scalar1=1.0 / D, scalar2=EPS, op0=ALU.mult,
                                                  op1=ALU.add), rd=[bss], wr=[bss])
            R.op("act", lambda e: e.activation(out=ss[:, 2:3], in_=ss[:, 1:2], func=AF.Sqrt), rd=[bss], wr=[bss])
            R.op("dve", lambda e: e.reciprocal(out=ss[:, 3:4], in_=ss[:, 2:3]), rd=[bss], wr=[bss])
            R.op("dve", lambda e: e.scalar_tensor_tensor(out=xt, in0=xt, scalar=ss[:, 3:4], in1=gf, op0=ALU.mult, op1=ALU.mult),
                 rd=[bxt, bss, b_gf], wr=[bxt])
            R.dma("sp", y_out[q * 128:(q + 1) * 128, :], xt, rd=[bxt], sb=bxt)
        R.barrier()

    stages = debug or "all"
    plan = {"g0": ["g0"], "g0f0": ["g0", "f0"], "l1m": ["g0", "f0", "a1"], "l1": ["g0", "f0", "a1", "f1"],
            "l2m": ["g0", "f0", "a1", "f1", "c2"], "l2": ["g0", "f0", "a1", "f1", "c2", "f2"],
            "all": ["g0", "f0", "a1", "f1", "c2", "f2", "g3", "f3"]}[stages]
    for ph in plan:
        if ph[0] == "g":
            phase_gmlp(int(ph[1]), int(ph[1]) // 3)
        elif ph[0] == "f":
            phase_ffn(int(ph[1]))
        elif ph[0] == "a":
            phase_attn(int(ph[1]))
        elif ph[0] == "c":
            phase_fft(int(ph[1]))
    if stages != "all":
        W = std_work(Carver(WREG, WORK_END))
        for q in range(NT):
            xt, bxt = W.xt[q % 2]
            R.dma("sp", xt, xs[q * 128:(q + 1) * 128, :], wr=[bxt], sb=bxt)
            R.dma("sp", dbg_out[q * 128:(q + 1) * 128, :], xt, rd=[bxt], sb=bxt)
        R.barrier()
    final_phase()
    return nc


def core_inputs(cfg, r, inp, x_all, c_all):
    TPC = cfg.TPC
    seq = 0 if r < 2 else (1 if r < 4 else 2)
    first = r in (0, 2, 4)
    last = r in (1, 3, 7)
    c = c_all[seq]
    m = {}
    m["x"] = np.ascontiguousarray(x_all[r * TPC:(r + 1) * TPC])
    m["cT"] = np.ascontiguousarray(c.reshape(8, 128).T)
    for kname in ("w_ada", "b_ada", "norm_g", "a_w_in", "a_g_v", "a_w_out", "c_w_out", "f_w_up", "f_w_down"):
        m[kname] = inp[kname]
    m["a_w_sT"] = np.ascontiguousarray(np.transpose(inp["a_w_s"], (0, 1, 3, 2)))
    m["a_b_sT"] = np.ascontiguousarray(np.transpose(inp["a_b_s"], (0, 2, 1)))
    m["b_w_qkv"] = inp["b_w_qkv"][0]
    m["b_sinks"] = inp["b_sinks"]
    m["b_w_o"] = inp["b_w_o"][0]
    m["c_w_inT"] = np.ascontiguousarray(inp["c_w_in"][0][:, r * 128:(r + 1) * 128].T)
    m["c_w_out"] = inp["c_w_out"][0]
    m["f_w_convT"] = np.ascontiguousarray(np.transpose(inp["f_w_conv"].reshape(4, 3, 44, 128), (0, 3, 1, 2)))
    m["f_b_convT"] = np.ascontiguousarray(np.transpose(inp["f_b_conv"].reshape(4, 44, 128), (0, 2, 1)))
    m["g_final"] = inp["g_final"].reshape(1, D)
    m["ident"] = bf(np.eye(128))
    m["m2"] = np.array([[0.0 if first else 1.0], [0.0 if last else 1.0]], np.float32)
    mk = np.zeros((128, 2), np.float32)
    mk[:, 0] = NEG if first else 0.0
    mk[:, 1] = NEG if last else 0.0
    m["mk"] = mk
    mm = np.ones((128, 2), np.float32)
    mm[:, 0] = 0.0 if first else 1.0
    mm[:, 1] = 0.0 if last else 1.0
    m["mm"] = mm
    qp = np.arange(128)[:, None]
    kp = np.arange(384)[None, :] - 128
    dist = np.abs(qp - kp).astype(np.float32)
    m["adist"] = dist
    m["amask"] = np.where(dist <= 128, 0.0, NEG).astype(np.float32)
    cc = np.arange(128, dtype=np.float64)
    ph = 2 * np.pi * ((cc[:, None] * cc[None, :]) % 128) / 128
    m["cs128"] = bf(np.concatenate([np.cos(ph), np.sin(ph)], axis=1))
    done = set()
    for (st, S) in cfg.seqs:
        if S in done:
            continue
        done.add(S)
        t2c, t2s, t1a, t1b = dft_tables(S)
        m["t2c_%d" % S], m["t2s_%d" % S], m["t1a_%d" % S], m["t1b_%d" % S] = t2c, t2s, t1a, t1b
    return m


def run(cfg, inp, x_all, c_all, debug=None):
    nc = build(cfg, debug)
    in_maps = [core_inputs(cfg, r, inp, x_all, c_all) for r in range(NCORE)]
    res = run_bass_kernel_spmd(nc, in_maps, core_ids=list(range(NCORE)))
    y = np.concatenate([res.results[r]["y"] for r in range(NCORE)], axis=0)
    dbg = None
    if debug:
        dbg = np.concatenate([res.results[r]["dbg"] for r in range(NCORE)], axis=0)
    return y, dbg


def kernel(**inputs):
    inp = {k: np.asarray(v) for k, v in inputs.items()}
    xp, xsm = inp["x_prompt"], inp["x_sample"]
    B, S, _ = xp.shape
    tpc = S // 2
    cfg = Cfg(tpc)
    x_all = np.concatenate([xp.reshape(B * S, D), xsm.reshape(-1, D)], axis=0)
    c_all = [inp["c_prompt"][0], inp["c_prompt"][1], inp["c_sample"][0]]
    y, _ = run(cfg, inp, x_all, c_all)
    y_prompt = y[:B * S].reshape(B, S, D).astype(np.float32)
    y_sample = y[B * S:].reshape(xsm.shape).astype(np.float32)
    return (y_prompt, y_sample)
```
